# Optimizing a Trainium2 kernel written in Bass

```python
import math
import jax
import jax.numpy as jnp
from jax import lax
import numpy as np

D_MODEL = 1024
BATCH = 16
SEQ = 4096
DEPTH = 4

GRID_W = 64
CTX_LEN = 256
N_MIXERS = 3
HEAD_DIM = 64
WA_HEADS = D_MODEL // HEAD_DIM
WA_KV_HEADS = WA_HEADS // 4
WA_GROUP = WA_HEADS // WA_KV_HEADS
WINDOW = 128
BLOCK = 128
DA_HEADS = D_MODEL // (2 * HEAD_DIM)
D_FF = 128 * ((8 * D_MODEL // 3 + 127) // 128)
CONV_W = 3
ROPE_BASE = 10000.0
EPS = 1e-6
F32 = jnp.float32

kernel_name = "hybrid_interleaved_dit_trunk"


def rmsnorm(x, g):
    xf = x.astype(F32)
    y = xf * lax.rsqrt(jnp.mean(xf * xf, axis=-1, keepdims=True) + EPS)
    return (y * g.astype(F32)).astype(x.dtype)


def modulate(xn, shift, scale):
    return xn * (1 + scale) + shift


def dwconv(x, w):
    ch = x.shape[-1]
    return lax.conv_general_dilated(
        x, w[:, None, :].astype(x.dtype), window_strides=(1,),
        padding=((CONV_W // 2, CONV_W // 2),),
        dimension_numbers=("NWC", "WIO", "NWC"), feature_group_count=ch)


def axial_rope_tables(n_tokens):
    rows = n_tokens // GRID_W
    row = jnp.repeat(jnp.arange(rows, dtype=F32), GRID_W)
    col = jnp.tile(jnp.arange(GRID_W, dtype=F32), rows)
    m = HEAD_DIM // 4
    inv_freq = ROPE_BASE ** (-jnp.arange(m, dtype=F32) / m)
    ang = jnp.stack([row, col], axis=-1)[:, :, None] * inv_freq
    return jnp.cos(ang), jnp.sin(ang)


def apply_rope(x, cos, sin):
    m = HEAD_DIM // 4
    bshape = (cos.shape[0],) + (1,) * (x.ndim - 3) + (2, m)
    cos, sin = cos.reshape(bshape), sin.reshape(bshape)
    xf = x.astype(F32).reshape(x.shape[:-1] + (2, 2, m))
    x1, x2 = xf[..., 0, :], xf[..., 1, :]
    out = jnp.stack([x1 * cos - x2 * sin, x2 * cos + x1 * sin], axis=-2)
    return out.reshape(x.shape).astype(x.dtype)


def short_conv_mixer(x, w_in, w_conv, w_out):
    b_gate, c_gate, h = jnp.split(x @ w_in, 3, axis=-1)
    return (b_gate * dwconv(c_gate * h, w_conv)) @ w_out


def window_gqa_mixer(x, xc, cos, sin, w_qkv, q_g, k_g, sink, w_out, ctx_out):
    bsz, n, _ = x.shape
    n_ctx = xc.shape[1]

    def project(t, rope):
        lead = t.shape[:2]
        q, k, v = jnp.split(t @ w_qkv, [WA_HEADS * HEAD_DIM, (WA_HEADS + WA_KV_HEADS) * HEAD_DIM], axis=-1)
        q = rmsnorm(q.reshape(lead + (WA_KV_HEADS, WA_GROUP, HEAD_DIM)), q_g)
        k = rmsnorm(k.reshape(lead + (WA_KV_HEADS, HEAD_DIM)), k_g)
        v = v.reshape(lead + (WA_KV_HEADS, HEAD_DIM))
        if rope:
            q, k = apply_rope(q, cos, sin), apply_rope(k, cos, sin)
        return q, k, v

    q, k, v = project(x, True)
    qc, kc, vc = project(xc, False)
    scale = HEAD_DIM ** -0.5
    sink_hg = sink.astype(F32).reshape(WA_KV_HEADS, WA_GROUP)[None, :, :, None, None]
    n_blocks = n // BLOCK
    span = BLOCK + 2 * WINDOW
    pad = ((0, 0), (WINDOW, WINDOW), (0, 0), (0, 0))
    k_pad, v_pad = jnp.pad(k, pad), jnp.pad(v, pad)
    q_blocks = jnp.moveaxis(q.reshape(bsz, n_blocks, BLOCK, WA_KV_HEADS, WA_GROUP, HEAD_DIM), 1, 0)

    def attend_block(args):
        q_j, j = args
        start = j * BLOCK
        k_j = lax.dynamic_slice_in_dim(k_pad, start, span, axis=1)
        v_j = lax.dynamic_slice_in_dim(v_pad, start, span, axis=1)
        q_pos = start + jnp.arange(BLOCK)
        k_pos = start - WINDOW + jnp.arange(span)
        mask = (jnp.abs(q_pos[:, None] - k_pos[None, :]) <= WINDOW) & (k_pos >= 0) & (k_pos < n)
        s_lat = jnp.einsum("bqhgd,bkhd->bhgqk", q_j, k_j, preferred_element_type=F32) * scale
        s_lat = jnp.where(mask, s_lat, -jnp.inf)
        s_ctx = jnp.einsum("bqhgd,bkhd->bhgqk", q_j, kc, preferred_element_type=F32) * scale
        s_sink = jnp.broadcast_to(sink_hg, s_ctx.shape[:-1] + (1,))
        p = jax.nn.softmax(jnp.concatenate([s_lat, s_ctx, s_sink], axis=-1), axis=-1).astype(v.dtype)
        return (jnp.einsum("bhgqk,bkhd->bqhgd", p[..., :span], v_j)
                + jnp.einsum("bhgqk,bkhd->bqhgd", p[..., span:span + n_ctx], vc))

    o = lax.map(attend_block, (q_blocks, jnp.arange(n_blocks)))
    y = jnp.moveaxis(o, 0, 1).reshape(bsz, n, WA_HEADS * HEAD_DIM) @ w_out
    if not ctx_out:
        return y, None
    s_c = jnp.einsum("bqhgd,bkhd->bhgqk", qc, kc, preferred_element_type=F32) * scale
    s_sink = jnp.broadcast_to(sink_hg, s_c.shape[:-1] + (1,))
    p_c = jax.nn.softmax(jnp.concatenate([s_c, s_sink], axis=-1), axis=-1)[..., :n_ctx].astype(vc.dtype)
    yc = jnp.einsum("bhgqk,bkhd->bqhgd", p_c, vc).reshape(bsz, n_ctx, WA_HEADS * HEAD_DIM) @ w_out
    return y, yc


def diff_attn_mixer(x, xc, cos, sin, w_qkv, q_g, k_g, lq1, lk1, lq2, lk2, sub_g, w_out, layer_idx, ctx_out):
    bsz, n, _ = x.shape
    n_ctx = xc.shape[1]
    lam_init = 0.8 - 0.6 * math.exp(-0.3 * layer_idx)
    lam = (jnp.exp(jnp.sum(lq1.astype(F32) * lk1.astype(F32)))
           - jnp.exp(jnp.sum(lq2.astype(F32) * lk2.astype(F32))) + lam_init)
    scale = HEAD_DIM ** -0.5

    def project(t, rope):
        lead = t.shape[:2]
        q, k, v = jnp.split(t @ w_qkv, 3, axis=-1)
        q = rmsnorm(q.reshape(lead + (DA_HEADS, 2, HEAD_DIM)), q_g)
        k = rmsnorm(k.reshape(lead + (DA_HEADS, 2, HEAD_DIM)), k_g)
        v = v.reshape(lead + (DA_HEADS, 2 * HEAD_DIM))
        if rope:
            q, k = apply_rope(q, cos, sin), apply_rope(k, cos, sin)
        return q, k, v

    def attend(q_j, keys, vals):
        s = jnp.einsum("bqhcd,bkhcd->bhcqk", q_j, keys, preferred_element_type=F32) * scale
        p = jax.nn.softmax(s, axis=-1)
        a = (p[:, :, 0] - lam * p[:, :, 1]).astype(vals.dtype)
        o = jnp.einsum("bhqk,bkhe->bqhe", a, vals)
        return rmsnorm(o, sub_g) * (1.0 - lam_init)

    q, k, v = project(x, True)
    qc, kc, vc = project(xc, False)
    k_all = jnp.concatenate([k, kc], axis=1)
    v_all = jnp.concatenate([v, vc], axis=1)
    n_blocks = n // BLOCK
    q_blocks = jnp.moveaxis(q.reshape(bsz, n_blocks, BLOCK, DA_HEADS, 2, HEAD_DIM), 1, 0)
    o = lax.map(lambda q_j: attend(q_j, k_all, v_all), q_blocks)
    y = jnp.moveaxis(o, 0, 1).reshape(bsz, n, D_MODEL) @ w_out
    if not ctx_out:
        return y, None
    yc = attend(qc, kc, vc).reshape(bsz, n_ctx, D_MODEL) @ w_out
    return y, yc


def conv_ffn(x, w_up, conv_w, conv_b, w_down):
    a, g = jnp.split(x @ w_up, 2, axis=-1)
    g = dwconv(g, conv_w) + conv_b
    return (jax.nn.silu(g) * a) @ w_down


def setup_inputs(seed: int = 0) -> dict:
    key = jax.random.key(seed)
    ks = iter(jax.random.split(key, 256))

    def nrm(shape, scale):
        return jax.random.normal(next(ks), shape, F32) * scale

    def gain(n):
        return 1.0 + nrm((n,), 0.02)

    D = D_MODEL
    p = {}
    p["x"] = nrm((BATCH, SEQ, D), 1.0)
    p["c"] = nrm((BATCH, D), 1.0)
    p["ctx"] = nrm((BATCH, CTX_LEN, D), 1.0)
    p["c_ctx"] = nrm((D,), 1.0)
    for i in range(DEPTH):
        kind = i % N_MIXERS
        pre = "l%d_" % i
        p[pre + "ada_w"] = nrm((D, 6 * D), 0.5 * D ** -0.5)
        p[pre + "ada_b"] = nrm((6 * D,), 0.02)
        p[pre + "norm1"] = gain(D)
        p[pre + "norm2"] = gain(D)
        if kind == 0:
            p[pre + "sc_in"] = nrm((D, 3 * D), D ** -0.5)
            p[pre + "sc_conv"] = nrm((CONV_W, D), CONV_W ** -0.5)
            p[pre + "sc_out"] = nrm((D, D), D ** -0.5)
        elif kind == 1:
            p[pre + "wa_qkv"] = nrm((D, (WA_HEADS + 2 * WA_KV_HEADS) * HEAD_DIM), D ** -0.5)
            p[pre + "wa_qnorm"] = gain(HEAD_DIM)
            p[pre + "wa_knorm"] = gain(HEAD_DIM)
            p[pre + "wa_sink"] = nrm((WA_HEADS,), 0.5)
            p[pre + "wa_out"] = nrm((WA_HEADS * HEAD_DIM, D), (WA_HEADS * HEAD_DIM) ** -0.5)
        else:
            p[pre + "da_qkv"] = nrm((D, 3 * D), D ** -0.5)
            p[pre + "da_qnorm"] = gain(HEAD_DIM)
            p[pre + "da_knorm"] = gain(HEAD_DIM)
            p[pre + "da_lq1"] = nrm((HEAD_DIM,), 0.1)
            p[pre + "da_lk1"] = nrm((HEAD_DIM,), 0.1)
            p[pre + "da_lq2"] = nrm((HEAD_DIM,), 0.1)
            p[pre + "da_lk2"] = nrm((HEAD_DIM,), 0.1)
            p[pre + "da_subln"] = gain(2 * HEAD_DIM)
            p[pre + "da_out"] = nrm((D, D), D ** -0.5)
        p[pre + "ffn_up"] = nrm((D, 2 * D_FF), D ** -0.5)
        p[pre + "ffn_conv_w"] = nrm((CONV_W, D_FF), CONV_W ** -0.5)
        p[pre + "ffn_conv_b"] = nrm((D_FF,), 0.02)
        p[pre + "ffn_down"] = nrm((D_FF, D), D_FF ** -0.5)
    return p


def reference(x, c, ctx, c_ctx,
              l0_ada_w, l0_ada_b, l0_norm1, l0_norm2, l0_sc_in, l0_sc_conv, l0_sc_out,
              l0_ffn_up, l0_ffn_conv_w, l0_ffn_conv_b, l0_ffn_down,
              l1_ada_w, l1_ada_b, l1_norm1, l1_norm2, l1_wa_qkv, l1_wa_qnorm, l1_wa_knorm, l1_wa_sink, l1_wa_out,
              l1_ffn_up, l1_ffn_conv_w, l1_ffn_conv_b, l1_ffn_down,
              l2_ada_w, l2_ada_b, l2_norm1, l2_norm2, l2_da_qkv, l2_da_qnorm, l2_da_knorm,
              l2_da_lq1, l2_da_lk1, l2_da_lq2, l2_da_lk2, l2_da_subln, l2_da_out,
              l2_ffn_up, l2_ffn_conv_w, l2_ffn_conv_b, l2_ffn_down,
              l3_ada_w, l3_ada_b, l3_norm1, l3_norm2, l3_sc_in, l3_sc_conv, l3_sc_out,
              l3_ffn_up, l3_ffn_conv_w, l3_ffn_conv_b, l3_ffn_down):
    cos, sin = axial_rope_tables(x.shape[1])
    commons = [(l0_ada_w, l0_ada_b, l0_norm1, l0_norm2),
               (l1_ada_w, l1_ada_b, l1_norm1, l1_norm2),
               (l2_ada_w, l2_ada_b, l2_norm1, l2_norm2),
               (l3_ada_w, l3_ada_b, l3_norm1, l3_norm2)]
    mixers = [(l0_sc_in, l0_sc_conv, l0_sc_out),
              (l1_wa_qkv, l1_wa_qnorm, l1_wa_knorm, l1_wa_sink, l1_wa_out),
              (l2_da_qkv, l2_da_qnorm, l2_da_knorm, l2_da_lq1, l2_da_lk1, l2_da_lq2, l2_da_lk2,
               l2_da_subln, l2_da_out),
              (l3_sc_in, l3_sc_conv, l3_sc_out)]
    ffns = [(l0_ffn_up, l0_ffn_conv_w, l0_ffn_conv_b, l0_ffn_down),
            (l1_ffn_up, l1_ffn_conv_w, l1_ffn_conv_b, l1_ffn_down),
            (l2_ffn_up, l2_ffn_conv_w, l2_ffn_conv_b, l2_ffn_down),
            (l3_ffn_up, l3_ffn_conv_w, l3_ffn_conv_b, l3_ffn_down)]

    h, hc = x, ctx
    for i in range(DEPTH):
        kind = i % N_MIXERS
        ada_w, ada_b, g1, g2 = commons[i]
        ctx_after = any(j % N_MIXERS != 0 for j in range(i + 1, DEPTH))
        ctx_here = ctx_after or kind != 0

        sh1, sc1, gt1, sh2, sc2, gt2 = [m[:, None, :] for m in jnp.split(jax.nn.silu(c) @ ada_w + ada_b, 6, axis=-1)]
        xn = modulate(rmsnorm(h, g1), sh1, sc1)
        xcn = None
        if ctx_here:
            ch1, cc1, cg1, ch2, cc2, cg2 = jnp.split(jax.nn.silu(c_ctx) @ ada_w + ada_b, 6, axis=-1)
            xcn = modulate(rmsnorm(hc, g1), ch1, cc1)

        if kind == 0:
            y = short_conv_mixer(xn, *mixers[i])
            yc = short_conv_mixer(xcn, *mixers[i]) if ctx_after else None
        elif kind == 1:
            y, yc = window_gqa_mixer(xn, xcn, cos, sin, *mixers[i], ctx_after)
        else:
            y, yc = diff_attn_mixer(xn, xcn, cos, sin, *mixers[i], i, ctx_after)

        h = h + gt1 * y
        h = h + gt2 * conv_ffn(modulate(rmsnorm(h, g2), sh2, sc2), *ffns[i])
        if ctx_after:
            hc = hc + cg1 * yc
            hc = hc + cg2 * conv_ffn(modulate(rmsnorm(hc, g2), ch2, cc2), *ffns[i])
    return h
```

```python
import math
import numpy as np
import concourse.bass as bass
import concourse.mybir as mybir
from concourse.bass_utils import run_bass_kernel_spmd

F32 = mybir.dt.float32
BF16 = mybir.dt.bfloat16
ALU = mybir.AluOpType
AF = mybir.ActivationFunctionType

D = 1024
KD = 8
SEQ = 4096
NB = 2
NCTX = 256
DFF = 2816
KF = 22
HD = 64
EPS = 1e-6
N_CORES = 8
T = 256
ENGS = ("pe", "act", "dve", "pool", "sp")
SEM_LIMIT = 30000


class Buf:
    __slots__ = ("name", "w", "r")

    def __init__(self, name=""):
        self.name = name
        self.w = None
        self.r = {}


def bufs(n, name=""):
    return [Buf("%s%d" % (name, i)) for i in range(n)]


class Ctx:
    N_DMA_SEMS = 32

    def __init__(self, nc):
        self.nc = nc
        self.epoch = {e: 0 for e in ENGS}
        self.semobj = {}
        self.cnt = {e: 0 for e in ENGS}
        self.pending = {e: False for e in ENGS}
        self.waited = {e: {} for e in ENGS}
        self.ops = {e: [] for e in ENGS}
        for e in ENGS:
            self.semobj[("e", e, 0)] = nc.alloc_semaphore("prog_%s_0" % e)
        self.dsem = [nc.alloc_semaphore("dma%d" % i) for i in range(self.N_DMA_SEMS)]
        self.dval = [0] * self.N_DMA_SEMS
        self.drr = 0
        for i, s in enumerate(self.dsem):
            self.semobj[("d", i)] = s
        self.n_ops = 0
        self.stage = "init"
        self.inst_stage = {}

    def _key(self, e):
        return ("e", e, self.epoch[e])

    def _deps(self, eng, reads, writes, same_engine_sync):
        deps = {}

        def add(t):
            if t is None:
                return
            k, v = t
            if deps.get(k, 0) < v:
                deps[k] = v
        for b in reads:
            add(b.w)
        for b in writes:
            add(b.w)
            for t in b.r.values():
                add(t)
        waits = []
        mykey = self._key(eng)
        for k, v in deps.items():
            if k == mykey:
                if not same_engine_sync or v > self.cnt[eng]:
                    continue
            if self.waited[eng].get(k, 0) >= v:
                continue
            self.waited[eng][k] = v
            waits.append((self.semobj[k], v))
        return waits

    def _mark(self, ticket, reads, writes):
        k = ticket[0]
        for b in reads:
            b.r[k] = ticket
        for b in writes:
            b.w = ticket
            b.r = {}

    def op(self, eng, fn, reads=(), writes=(), inc=True, sync_same=None):
        if sync_same is None:
            sync_same = (eng != "pe")
        if inc and self.cnt[eng] >= SEM_LIMIT and not self.pending[eng]:
            self.epoch[eng] += 1
            self.cnt[eng] = 0
            self.semobj[self._key(eng)] = self.nc.alloc_semaphore(
                "prog_%s_%d" % (eng, self.epoch[eng]))
        waits = self._deps(eng, reads, writes, sync_same)
        key = self._key(eng)
        if inc:
            self.cnt[eng] += 1
            ticket = (key, self.cnt[eng])
            self.pending[eng] = False
            incspec = (self.semobj[key], 1)
        else:
            ticket = (key, self.cnt[eng] + 1)
            self.pending[eng] = True
            incspec = None
        self.ops[eng].append((waits, fn, incspec, self.stage))
        self._mark(ticket, reads, writes)
        self.n_ops += 1
        return ticket

    def I(self, eng, meth, kw, reads=(), writes=(), inc=True, sync_same=None):
        def fn(h, meth=meth, kw=kw):
            return getattr(h, meth)(**kw)
        return self.op(eng, fn, reads=reads, writes=writes, inc=inc, sync_same=sync_same)

    def dma(self, out_ap, in_ap, reads=(), writes=(), q="sp"):
        i = self.drr
        self.drr = (self.drr + 1) % self.N_DMA_SEMS
        waits = self._deps(q, reads, writes, True)
        k = ("d", i)
        if self.dval[i] > 0 and self.waited[q].get(k, 0) < self.dval[i]:
            self.waited[q][k] = self.dval[i]
            waits.append((self.dsem[i], self.dval[i]))
        self.dval[i] += 16
        ticket = (k, self.dval[i])

        def fn(h, out_ap=out_ap, in_ap=in_ap):
            return h.dma_start(out=out_ap, in_=in_ap)
        self.ops[q].append((waits, fn, (self.dsem[i], 16), self.stage))
        self._mark(ticket, reads, writes)
        self.n_ops += 1
        return ticket

    def wait_all(self, eng):
        waits = []
        for (k, s) in list(self.semobj.items()):
            if k[0] == "e":
                e = k[1]
                if e == eng:
                    continue
                if k[2] == self.epoch[e]:
                    assert not self.pending[e], "pending non-inc op on " + e
                    v = self.cnt[e]
                else:
                    v = SEM_LIMIT
            else:
                v = self.dval[k[1]]
            if v > 0 and self.waited[eng].get(k, 0) < v:
                self.waited[eng][k] = v
                waits.append((s, v))
        if waits:
            self.ops[eng].append((waits, None, None, self.stage))

    def barrier(self):
        for e in ENGS:
            self.wait_all(e)

    def emit(self):
        nc = self.nc
        self.wait_all("sp")
        h = {"pe": nc.tensor, "act": nc.scalar, "dve": nc.vector, "pool": nc.gpsimd,
             "sp": nc.sync}
        with nc.Block() as block:
            def run(e):
                def body(hh):
                    for waits, fn, incspec, stg in self.ops[e]:
                        for s, v in waits:
                            hh.wait_ge(s, v)
                        if fn is None:
                            continue
                        ins = fn(hh)
                        try:
                            self.inst_stage[ins.ins.name] = stg
                        except Exception:
                            pass
                        if incspec is not None:
                            ins.then_inc(incspec[0], incspec[1])
                return body
            block.tensor(run("pe"))
            block.scalar(run("act"))
            block.vector(run("dve"))
            block.gpsimd(run("pool"))
            block.sync(run("sp"))


class Arena:
    LO = 16896
    HI = 229376

    def __init__(self, nc):
        self.nc = nc
        self.top = self.LO
        self.n = 0

    def alloc(self, name, shape, dtype):
        nbytes = int(np.prod(shape[1:])) * (2 if dtype == BF16 else 4)
        off = (self.top + 63) // 64 * 64
        assert off + nbytes <= self.HI, "SBUF overflow at %s: %d" % (name, off + nbytes)
        self.top = off + nbytes
        self.n += 1
        return self.nc.alloc_sbuf_tensor_at("%s_%d" % (name, self.n), list(shape), dtype, offset=off)

    def mark(self):
        return self.top

    def reset(self, m):
        self.top = m


def _layer_kind(l):
    return l % 3


def _cvec_layout():
    off = {}
    n = 0

    def add(name, w):
        nonlocal n
        off[name] = (n, w)
        n += w
    for l in range(4):
        p = "l%d_" % l
        add(p + "ada_b", 48)
        add(p + "norm1", 8)
        add(p + "norm2", 8)
        add(p + "ffn_conv_w", 3 * KF)
        add(p + "ffn_conv_b", KF)
        k = _layer_kind(l)
        if k == 0:
            add(p + "sc_conv", 3 * KD)
        elif k == 1:
            add(p + "qg", 1)
            add(p + "kg", 1)
            add(p + "sink", 16)
        else:
            add(p + "qg", 1)
            add(p + "kg", 1)
            for v in ("lq1", "lk1", "lq2", "lk2"):
                add(p + v, 64)
            add(p + "subln", 1)
    return off, n


def _pk(v, k):
    return np.ascontiguousarray(v.reshape(k, 128).T)


def _pack_cvec(inp):
    off, n = _cvec_layout()
    cv = np.zeros((128, n), np.float32)

    def put(name, arr):
        o, w = off[name]
        assert arr.shape == (128, w), (name, arr.shape, w)
        cv[:, o:o + w] = arr
    for l in range(4):
        p = "l%d_" % l
        put(p + "ada_b", _pk(inp[p + "ada_b"], 48))
        put(p + "norm1", _pk(inp[p + "norm1"], 8))
        put(p + "norm2", _pk(inp[p + "norm2"], 8))
        cw = inp[p + "ffn_conv_w"]
        put(p + "ffn_conv_w", np.concatenate([_pk(cw[t], KF) for t in range(3)], axis=1))
        put(p + "ffn_conv_b", _pk(inp[p + "ffn_conv_b"], KF))
        k = _layer_kind(l)
        if k == 0:
            cw = inp[p + "sc_conv"]
            put(p + "sc_conv", np.concatenate([_pk(cw[t], KD) for t in range(3)], axis=1))
        elif k == 1:
            put(p + "qg", np.tile(inp[p + "wa_qnorm"], 2)[:, None])
            put(p + "kg", np.tile(inp[p + "wa_knorm"], 2)[:, None])
            put(p + "sink", np.broadcast_to(inp[p + "wa_sink"][None, :], (128, 16)))
        else:
            put(p + "qg", np.tile(inp[p + "da_qnorm"], 2)[:, None])
            put(p + "kg", np.tile(inp[p + "da_knorm"], 2)[:, None])
            for v in ("lq1", "lk1", "lq2", "lk2"):
                put(p + v, np.broadcast_to(inp[p + "da_" + v][None, :], (128, 64)))
            put(p + "subln", inp[p + "da_subln"][:, None])
    return cv


def _rope_tables():
    rows = SEQ // 64
    row = np.repeat(np.arange(rows, dtype=np.float32), 64)
    col = np.tile(np.arange(64, dtype=np.float32), rows)
    m = HD // 4
    inv_freq = (np.float32(10000.0) ** (-np.arange(m, dtype=np.float32) / np.float32(m))).astype(np.float32)
    cos = np.zeros((128, SEQ), np.float32)
    sin = np.zeros((128, SEQ), np.float32)
    for p in range(128):
        loc = p % 64
        axis = loc // 32
        mm = loc % 16
        ang = (row if axis == 0 else col) * inv_freq[mm]
        cos[p] = np.cos(ang)
        sin[p] = np.sin(ang)
    R = np.zeros((128, 128), np.float32)
    for dst in range(128):
        loc = dst % 64
        half = (loc % 32) // 16
        if half == 0:
            R[dst + 16, dst] = -1.0
        else:
            R[dst - 16, dst] = 1.0
    return cos, sin, R


def _band_masks():
    k = np.arange(128)[:, None]
    q = np.arange(128)[None, :]
    prev = (k >= q).astype(np.float32)
    nxt = (k <= q).astype(np.float32)
    return np.ascontiguousarray(np.concatenate([np.tile(prev, (1, 4)), np.tile(nxt, (1, 4))], axis=1))


def build_program(layers=(0, 1, 2, 3), debug=False, stop=None):
    nc = bass.Bass("TRN2", target_bir_lowering=False)
    c = Ctx(nc)
    ar = Arena(nc)
    NT = NB * SEQ
    NC_TOK = NB * NCTX
    cvoff, ncv = _cvec_layout()

    def din(name, shape, dt=F32):
        return nc.dram_tensor(name, list(shape), dt, kind="ExternalInput").ap()

    xT = din("xT", [D, NT])
    ctxT = din("ctxT", [D, NC_TOK])
    c3T = din("c3T", [D, 4])
    cvec_d = din("cvec", [128, ncv])
    W = {}
    for l in range(4):
        p = "l%d_" % l
        W[p + "ada_w"] = din(p + "ada_w", [D, 6 * D])
        k = _layer_kind(l)
        if k == 0:
            W[p + "sc_in"] = din(p + "sc_in", [D, 3 * D])
            W[p + "sc_out"] = din(p + "sc_out", [D, D])
        elif k == 1:
            W[p + "wa_qkv"] = din(p + "wa_qkv", [D, 1536])
            W[p + "wa_out"] = din(p + "wa_out", [D, D])
        else:
            W[p + "da_qkv"] = din(p + "da_qkv", [D, 3 * D])
            W[p + "da_out"] = din(p + "da_out", [D, D])
        W[p + "ffn_up"] = din(p + "ffn_up", [D, 2 * DFF])
        W[p + "ffn_down"] = din(p + "ffn_down", [DFF, D])
    cos_d = din("cosT", [128, SEQ])
    sin_d = din("sinT", [128, SEQ])
    rperm_d = din("rperm", [128, 128])
    mask_d = din("masks", [128, 2 * 512])
    QT_lat = nc.dram_tensor("QT_lat", [D, NT], BF16).ap()
    QT_ctx = nc.dram_tensor("QT_ctx", [D, NC_TOK], BF16).ap()
    KT_lat = nc.dram_tensor("KT_lat", [D, NT], BF16).ap()
    KT_ctx = nc.dram_tensor("KT_ctx", [D, NC_TOK], BF16).ap()
    V_lat = nc.dram_tensor("V_lat", [NT, D], BF16).ap()
    V_ctx = nc.dram_tensor("V_ctx", [NC_TOK, D], BF16).ap()
    AT_lat = nc.dram_tensor("AT_lat", [D, NT], BF16).ap()
    AT_ctx = nc.dram_tensor("AT_ctx", [D, NC_TOK], BF16).ap()
    yT = nc.dram_tensor("yT", [D, NT], F32, kind="ExternalOutput").ap()
    if debug:
        ycT = nc.dram_tensor("ycT", [D, NC_TOK], F32, kind="ExternalOutput").ap()
        modo = nc.dram_tensor("modo", [128, 4 * 6 * KD * 4], F32, kind="ExternalOutput").ap()
    hbuf = [nc.dram_tensor("hs%d" % i, [D, NT], F32).ap() for i in range(2)]
    hcbuf = [nc.dram_tensor("hcs%d" % i, [D, NC_TOK], F32).ap() for i in range(2)]

    cv = ar.alloc("cv", [128, ncv], F32)
    Bcv = Buf("cv")
    ones_bf = ar.alloc("ones", [128, 128], BF16)
    Bones = Buf("ones")
    mod = ar.alloc("mod", [128, 4 * 6 * KD * 4], F32)
    modA = ar.alloc("modA", [128, 4 * 2 * KD * 4], F32)
    Bmod = Buf("mod")
    psum = nc.alloc_psum_tensor("ps", [128, 8, 512], F32)
    PB = bufs(8, "psb")

    def cvc(name, j=0, w=1):
        o, _ = cvoff[name]
        return cv[:, o + j:o + j + w]

    def mod_ap(l, j, k, tok):
        i = ((l * 6 + j) * KD + k) * 4 + tok
        return mod[:, i:i + 1]

    def modA_ap(l, which, k, tok):
        i = ((l * 2 + which) * KD + k) * 4 + tok
        return modA[:, i:i + 1]

    c.dma(cv[:], cvec_d, writes=[Bcv])
    c.I("pool", "memset", dict(ap=ones_bf[:], constant=1.0), writes=[Bones])
    c.I("pool", "memset", dict(ap=mod[:], constant=0.0), writes=[Bmod])
    c.I("pool", "memset", dict(ap=modA[:], constant=0.0), writes=[Bmod])

    persist_mark = ar.mark()

    cast_rr = [0]

    def cast_op(dst_ap, src_ap, reads, writes):
        e = ("dve", "act", "pool")[cast_rr[0] % 3]
        cast_rr[0] += 1
        if e == "act":
            c.I("act", "activation", dict(out=dst_ap, in_=src_ap, func=AF.Copy), reads=reads, writes=writes)
        else:
            c.I(e, "tensor_copy", dict(out=dst_ap, in_=src_ap), reads=reads, writes=writes)

    def load_weight(dst, Bdst, src, K, N, stg, Bstg, cols=2048):
        i = 0
        for k in range(K):
            for n0 in range(0, N, cols):
                n1 = min(N, n0 + cols)
                s = i % len(stg)
                i += 1
                c.dma(stg[s][:, 0:n1 - n0], src[k * 128:(k + 1) * 128, n0:n1], writes=[Bstg[s]])
                cast_op(dst[:, k, n0:n1], stg[s][:, 0:n1 - n0], [Bstg[s]], [Bdst])

    def mm_group(out_ap, Bout, terms, extra_reads=()):
        n = len(terms)
        for i, (l_ap, r_ap, rb) in enumerate(terms):
            c.I("pe", "matmul", dict(out=out_ap, lhsT=l_ap, rhs=r_ap, start=(i == 0), stop=(i == n - 1)),
                 reads=list(rb) + list(extra_reads), writes=[Bout], inc=(i == n - 1))

    def stage_ada():
        c.stage = "ada"
        m0 = ar.mark()
        sc = ar.alloc("silu_c", [128, KD, 4], F32)
        Bsc = Buf("sc")
        c3 = ar.alloc("c3", [128, KD, 4], F32)
        Bc3 = Buf("c3")
        wst = [ar.alloc("adaw%d" % i, [128, KD, 1024], F32) for i in range(2)]
        Bw = bufs(2, "adaw")
        c.dma(c3[:], c3T.rearrange("(k p) t -> p k t", p=128), writes=[Bc3])
        c.I("act", "activation", dict(out=sc[:], in_=c3[:], func=AF.Silu), reads=[Bc3], writes=[Bsc])
        it = 0
        for l in layers:
            aw = W["l%d_ada_w" % l]
            for j in range(6):
                s = it % 2
                it += 1
                c.dma(wst[s][:], aw[:, j * 1024:(j + 1) * 1024].rearrange("(k p) n -> p k n", p=128), writes=[Bw[s]])
                for k in range(KD):
                    bank = (it * KD + k) % 8
                    mm_group(psum[:, bank, 0:4], PB[bank],
                             [(wst[s][:, kk, k * 128:(k + 1) * 128], sc[:, kk, :], [Bw[s], Bsc]) for kk in range(KD)])
                    i0 = ((l * 6 + j) * KD + k) * 4
                    o, _ = cvoff["l%d_ada_b" % l]
                    c.I("dve", "tensor_scalar", dict(
                        out=mod[:, i0:i0 + 4], in0=psum[:, bank, 0:4], scalar1=cv[:, o + j * 8 + k:o + j * 8 + k + 1], scalar2=None, op0=ALU.add),
                        reads=[PB[bank], Bcv], writes=[Bmod])
            for which, (gname, jsc) in enumerate((("norm1", 1), ("norm2", 4))):
                go, _ = cvoff["l%d_%s" % (l, gname)]
                for k in range(KD):
                    i0 = ((l * 6 + jsc) * KD + k) * 4
                    a0 = ((l * 2 + which) * KD + k) * 4
                    c.I("dve", "tensor_scalar", dict(
                        out=modA[:, a0:a0 + 4], in0=mod[:, i0:i0 + 4], scalar1=1.0, scalar2=cv[:, go + k:go + k + 1],
                        op0=ALU.add, op1=ALU.mult), reads=[Bmod, Bcv], writes=[Bmod])
                    c.I("dve", "tensor_scalar", dict(
                        out=modA[:, a0:a0 + 4], in0=modA[:, a0:a0 + 4], scalar1=32.0, scalar2=None, op0=ALU.mult),
                        reads=[Bmod], writes=[Bmod])
        if debug:
            c.dma(modo, mod[:], reads=[Bmod])
        c.barrier()
        ar.reset(m0)

    def make_tiles(with_ctx):
        tiles = []
        for b in range(NB):
            for i in range(SEQ // T):
                tiles.append(dict(kind="lat", tok=b, g0=b * SEQ + i * T, first=(i == 0), last=(i == SEQ // T - 1)))
        if with_ctx:
            for b in range(NB):
                tiles.append(dict(kind="ctx", tok=2, g0=b * NCTX, first=True, last=True))
        return tiles

    WH = T + 2

    class NormPipe:
        def __init__(self, l, which, src_lat, src_ctx, nh, nx, Wt=WH, halo=True, tt_eng="dve"):
            self.l, self.which = l, which
            self.tt_eng = tt_eng
            self.W, self.halo = Wt, halo
            self.src = {"lat": src_lat, "ctx": src_ctx}
            self.h = [ar.alloc("h%d" % i, [128, KD, Wt], F32) for i in range(nh)]
            self.Bh = bufs(nh, "h")
            self.xn = [ar.alloc("xn%d" % i, [128, KD, Wt], BF16) for i in range(nx)]
            self.Bxn = [bufs(KD, "xn%d_" % i) for i in range(nx)]
            self.sq = [ar.alloc("sq%d" % i, [128, Wt], BF16) for i in range(4)]
            self.Bsq = bufs(4, "sq")
            self.rs = [ar.alloc("rs%d" % i, [128, Wt], F32) for i in range(2)]
            self.Brs = bufs(2, "rs")
            self.tt = [ar.alloc("tt%d" % i, [128, Wt], F32) for i in range(2)]
            self.Btt = bufs(2, "tt")
            self.nsq = 0
            self.ntt = 0

        def load(self, i, t):
            hb, B = self.h[i % len(self.h)], self.Bh[i % len(self.h)]
            src = self.src[t["kind"]]
            Wt = self.W
            if not self.halo:
                c.dma(hb[:], src[:, t["g0"]:t["g0"] + Wt].rearrange("(k p) n -> p k n", p=128), writes=[B])
                return
            lo = 1 if t["first"] else 0
            hi = Wt - 1 if t["last"] else Wt
            g0 = t["g0"] - 1
            c.dma(hb[:, :, lo:hi], src[:, g0 + lo:g0 + hi].rearrange("(k p) n -> p k n", p=128), writes=[B])
            if t["first"]:
                c.I("pool", "memset", dict(ap=hb[:, :, 0:1], constant=0.0), writes=[B])
            if t["last"]:
                c.I("pool", "memset", dict(ap=hb[:, :, Wt - 1:Wt], constant=0.0), writes=[B])

        def sq_k(self, i, k):
            hb, B = self.h[i % len(self.h)], self.Bh[i % len(self.h)]
            s = k % 4
            c.I("act", "activation", dict(out=self.sq[s][:], in_=hb[:, k, :], func=AF.Square),
                reads=[B], writes=[self.Bsq[s]])

        def ss_k(self, i, k, ssbank):
            s = k % 4
            c.I("pe", "matmul", dict(out=psum[:, ssbank, 0:self.W], lhsT=ones_bf[:], rhs=self.sq[s][:],
                                     start=(k == 0), stop=(k == KD - 1)),
                reads=[self.Bsq[s], Bones], writes=[PB[ssbank]], inc=True)

        def rs_fin(self, i, ssbank):
            r = i % 2
            c.I("act", "activation", dict(out=self.rs[r][:], in_=psum[:, ssbank, 0:self.W], func=AF.Ln, bias=EPS * D, scale=1.0),
                reads=[PB[ssbank]], writes=[self.Brs[r]])
            c.I("act", "activation", dict(out=self.rs[r][:], in_=self.rs[r][:], func=AF.Exp, scale=-0.5),
                reads=[self.Brs[r]], writes=[self.Brs[r]])

        def part1(self, i, t, ssbank):
            for k in range(KD):
                self.sq_k(i, k)
                self.ss_k(i, k, ssbank)
            self.rs_fin(i, ssbank)

        def part2_k(self, i, t, k):
            hb, B = self.h[i % len(self.h)], self.Bh[i % len(self.h)]
            x, Bx = self.xn[i % len(self.xn)], self.Bxn[i % len(self.xn)]
            r = i % 2
            jsh = 0 if self.which == 0 else 3
            s = k % 2
            c.I(self.tt_eng, "tensor_tensor", dict(out=self.tt[s][:], in0=hb[:, k, :], in1=self.rs[r][:], op=ALU.mult),
                reads=[B, self.Brs[r]], writes=[self.Btt[s]])
            c.I("act", "activation", dict(out=x[:, k, :], in_=self.tt[s][:], func=AF.Identity,
                                          scale=modA_ap(self.l, self.which, k, t["tok"]),
                                          bias=mod_ap(self.l, jsh, k, t["tok"])),
                reads=[self.Btt[s], Bmod], writes=[Bx[k]])

        def part2(self, i, t):
            for k in range(KD):
                self.part2_k(i, t, k)

        def spread(self, step, i, t, ssbank, per=1, start=0):
            step -= start
            if step < 0:
                return
            n1 = KD // per
            if 1 <= step < n1 + 1:
                for k in range((step - 1) * per, step * per):
                    self.ss_k(i, k, ssbank)
            if step < n1:
                for k in range(step * per, (step + 1) * per):
                    self.sq_k(i, k)
            if step == n1 + 1:
                self.rs_fin(i, ssbank)
            if n1 + 2 <= step < 2 * n1 + 2:
                for k in range((step - n1 - 2) * per, (step - n1 - 1) * per):
                    self.part2_k(i, t, k)

        def hbuf(self, i):
            return self.h[i % len(self.h)], self.Bh[i % len(self.h)]

        def xnbuf(self, i):
            return self.xn[i % len(self.xn)], self.Bxn[i % len(self.xn)]

    def stage_sconv(l, src_lat, src_ctx, dst_lat, dst_ctx, with_ctx):
        c.stage = "L%d_sconv" % l
        m0 = ar.mark()
        p = "l%d_" % l
        win = ar.alloc("win", [128, KD, 3 * D], BF16)
        wout = ar.alloc("wout", [128, KD, D], BF16)
        Bwin, Bwout = Buf("win"), Buf("wout")
        m1 = ar.mark()
        stg = [ar.alloc("stg%d" % i, [128, 2048], F32) for i in range(4)]
        Bstg = bufs(4, "stg")
        load_weight(win, Bwin, W[p + "sc_in"], KD, 3 * D, stg, Bstg)
        load_weight(wout, Bwout, W[p + "sc_out"], KD, D, stg, Bstg)
        c.barrier()
        ar.reset(m1)
        npipe = NormPipe(l, 0, src_lat, src_ctx, nh=3, nx=2)
        PH7 = bufs(2, "ps7h")
        cgs = [ar.alloc("cgs%d" % i, [128, WH], F32) for i in range(2)]
        Bcgs = bufs(2, "cgs")
        ch = [ar.alloc("ch%d" % i, [128, WH], F32) for i in range(2)]
        Bch = bufs(2, "ch")
        acc = [ar.alloc("acc%d" % i, [128, T], F32) for i in range(2)]
        Bacc = bufs(2, "acc")
        v = [ar.alloc("v%d" % i, [128, KD, T], BF16) for i in range(2)]
        Bv = [bufs(KD, "v%d_" % i) for i in range(2)]
        yo = [ar.alloc("yo%d" % i, [128, KD, T], F32) for i in range(2)]
        Byo = bufs(2, "yo")
        tiles = make_tiles(with_ctx)
        dst = {"lat": dst_lat, "ctx": dst_ctx}
        cwo, _ = cvoff[p + "sc_conv"]
        SS = 0

        def cw(tap, m):
            return cv[:, cwo + tap * KD + m:cwo + tap * KD + m + 1]

        n = len(tiles)
        npipe.load(0, tiles[0])
        if n > 1:
            npipe.load(1, tiles[1])
        npipe.part1(0, tiles[0], SS)
        npipe.part2(0, tiles[0])
        it = 0
        for i, t in enumerate(tiles):
            if i + 2 < n:
                npipe.load(i + 2, tiles[i + 2])
            x, Bx = npipe.xnbuf(i)
            hb, Bh = npipe.hbuf(i)
            vv, Bvv = v[i % 2], Bv[i % 2]
            def sc_p1(m):
                nonlocal it
                st = it % 2
                it += 1
                bcg, bhh, bb = 1 + 3 * st, 2 + 3 * st, 3 + 3 * st
                s = m % 2
                mm_group(psum[:, bcg, 0:WH], PB[bcg],
                         [(win[:, k, D + m * 128:D + (m + 1) * 128], x[:, k, :], [Bwin, Bx[k]]) for k in range(KD)])
                mm_group(psum[:, bhh, 0:WH], PB[bhh],
                         [(win[:, k, 2 * D + m * 128:2 * D + (m + 1) * 128], x[:, k, :], [Bwin, Bx[k]]) for k in range(KD)])
                c.I("act", "activation", dict(out=cgs[s][:], in_=psum[:, bcg, 0:WH], func=AF.Copy),
                    reads=[PB[bcg]], writes=[Bcgs[s]])
                c.I("dve", "tensor_tensor", dict(out=ch[s][:], in0=cgs[s][:], in1=psum[:, bhh, 0:WH], op=ALU.mult),
                    reads=[Bcgs[s], PB[bhh]], writes=[Bch[s]])
                if t["first"]:
                    c.I("pool", "memset", dict(ap=ch[s][:, 0:1], constant=0.0), writes=[Bch[s]])
                if t["last"]:
                    c.I("pool", "memset", dict(ap=ch[s][:, WH - 1:WH], constant=0.0), writes=[Bch[s]])
                mm_group(psum[:, bb, 0:T], PB[bb],
                         [(win[:, k, m * 128:(m + 1) * 128], x[:, k, 1:T + 1], [Bwin, Bx[k]]) for k in range(KD)])
                return bb

            def sc_p2(m, bb):
                s = m % 2
                c.I("act", "activation", dict(out=acc[s][:], in_=ch[s][:, 0:T], func=AF.Copy, scale=cw(0, m)),
                    reads=[Bch[s], Bcv], writes=[Bacc[s]])
                c.I("dve", "scalar_tensor_tensor", dict(out=acc[s][:], in0=ch[s][:, 1:T + 1], scalar=cw(1, m), in1=acc[s][:],
                                                        op0=ALU.mult, op1=ALU.add),
                    reads=[Bch[s], Bacc[s], Bcv], writes=[Bacc[s]])
                c.I("dve", "scalar_tensor_tensor", dict(out=acc[s][:], in0=ch[s][:, 2:T + 2], scalar=cw(2, m), in1=acc[s][:],
                                                        op0=ALU.mult, op1=ALU.add),
                    reads=[Bch[s], Bacc[s], Bcv], writes=[Bacc[s]])
                c.I("dve", "tensor_tensor", dict(out=vv[:, m, :], in0=acc[s][:], in1=psum[:, bb, 0:T], op=ALU.mult),
                    reads=[Bacc[s], PB[bb]], writes=[Bvv[m]])

            prev = None
            for m in range(KD + 1):
                if i + 1 < n:
                    npipe.spread(m, i + 1, tiles[i + 1], SS, per=4, start=0)
                cur = None
                if m < KD:
                    cur = (m, sc_p1(m))
                if prev is not None:
                    sc_p2(*prev)
                prev = cur
            yy, Byy = yo[i % 2], Byo[i % 2]
            for nn in range(KD):
                bo = (7, 1, 2, 3, 4, 5, 6, 7)[nn]
                mm_group(psum[:, bo, 0:T], PB[bo],
                         [(wout[:, m, nn * 128:(nn + 1) * 128], vv[:, m, :], [Bwout, Bvv[m]]) for m in range(KD)])
                c.I("dve", "scalar_tensor_tensor", dict(out=yy[:, nn, :], in0=psum[:, bo, 0:T], scalar=mod_ap(l, 2, nn, t["tok"]),
                                                                      in1=hb[:, nn, 1:T + 1], op0=ALU.mult, op1=ALU.add),
                     reads=[PB[bo], Bh, Bmod], writes=[Byy])
            c.dma(dst[t["kind"]][:, t["g0"]:t["g0"] + T].rearrange("(k p) n -> p k n", p=128), yy[:], reads=[Byy])
        c.barrier()
        ar.reset(m0)

    def stage_ffn(l, src_lat, src_ctx, dst_lat, dst_ctx, with_ctx):
        c.stage = "L%d_ffn" % l
        m0 = ar.mark()
        p = "l%d_" % l
        wup = ar.alloc("wup", [128, KD, 2 * DFF], BF16)
        wdn = ar.alloc("wdn", [128, KF, D], BF16)
        Bwup, Bwdn = Buf("wup"), Buf("wdn")
        m1 = ar.mark()
        stg = [ar.alloc("stg%d" % i, [128, 2048], F32) for i in range(4)]
        Bstg = bufs(4, "stg")
        load_weight(wup, Bwup, W[p + "ffn_up"], KD, 2 * DFF, stg, Bstg)
        load_weight(wdn, Bwdn, W[p + "ffn_down"], KF, D, stg, Bstg)
        c.barrier()
        ar.reset(m1)
        npipe = NormPipe(l, 1, src_lat, src_ctx, nh=2, nx=2, tt_eng="pool")
        t0 = [ar.alloc("t0_%d" % i, [128, T], F32) for i in range(2)]
        Bt0 = bufs(2, "t0")
        sg = [ar.alloc("sg%d" % i, [128, T], F32) for i in range(2)]
        Bsg = bufs(2, "sg")
        u = [ar.alloc("u%d" % i, [128, KF, T], BF16) for i in range(2)]
        Bu = [bufs(KF, "u%d_" % i) for i in range(2)]
        yo = [ar.alloc("yo%d" % i, [128, KD, T], F32) for i in range(1)]
        Byo = bufs(1, "yo")
        tiles = make_tiles(with_ctx)
        dst = {"lat": dst_lat, "ctx": dst_ctx}
        cwo, _ = cvoff[p + "ffn_conv_w"]
        cbo, _ = cvoff[p + "ffn_conv_b"]
        SS = 6

        def cw(tap, j):
            return cv[:, cwo + tap * KF + j:cwo + tap * KF + j + 1]

        n = len(tiles)
        npipe.load(0, tiles[0])
        if n > 1:
            npipe.load(1, tiles[1])
        npipe.part1(0, tiles[0], SS)
        npipe.part2(0, tiles[0])
        it = 0
        for i, t in enumerate(tiles):
            x, Bx = npipe.xnbuf(i)
            hb, Bh = npipe.hbuf(i)
            uu, Buu = u[i % 2], Bu[i % 2]
            def ffn_p1(j):
                nonlocal it
                st = it % 3
                it += 1
                bg = 1 + st
                ba = (0, 4, 5)[it % 3]
                s = j % 2
                mm_group(psum[:, bg, 0:WH], PB[bg],
                         [(wup[:, k, DFF + j * 128:DFF + (j + 1) * 128], x[:, k, :], [Bwup, Bx[k]]) for k in range(KD)])
                if t["first"]:
                    c.I("dve", "memset", dict(ap=psum[:, bg, 0:1], constant=0.0), writes=[PB[bg]])
                if t["last"]:
                    c.I("dve", "memset", dict(ap=psum[:, bg, WH - 1:WH], constant=0.0), writes=[PB[bg]])
                c.I("act", "activation", dict(out=t0[s][:], in_=psum[:, bg, 0:T], func=AF.Copy, scale=cw(0, j)),
                    reads=[PB[bg], Bcv], writes=[Bt0[s]])
                c.I("dve", "scalar_tensor_tensor", dict(out=t0[s][:], in0=psum[:, bg, 1:T + 1], scalar=cw(1, j), in1=t0[s][:],
                                                        op0=ALU.mult, op1=ALU.add),
                    reads=[PB[bg], Bt0[s], Bcv], writes=[Bt0[s]])
                c.I("dve", "scalar_tensor_tensor", dict(out=t0[s][:], in0=psum[:, bg, 2:T + 2], scalar=cw(2, j), in1=t0[s][:],
                                                        op0=ALU.mult, op1=ALU.add),
                    reads=[PB[bg], Bt0[s], Bcv], writes=[Bt0[s]])
                mm_group(psum[:, ba, 0:T], PB[ba],
                         [(wup[:, k, j * 128:(j + 1) * 128], x[:, k, 1:T + 1], [Bwup, Bx[k]]) for k in range(KD)])
                return ba

            def ffn_p2(j, ba):
                s = j % 2
                s2 = j % 2
                c.I("act", "activation", dict(out=sg[s2][:], in_=t0[s][:], func=AF.Silu, bias=cv[:, cbo + j:cbo + j + 1]),
                    reads=[Bt0[s], Bcv], writes=[Bsg[s2]])
                c.I("dve", "tensor_tensor", dict(out=uu[:, j, :], in0=sg[s2][:], in1=psum[:, ba, 0:T], op=ALU.mult),
                    reads=[Bsg[s2], PB[ba]], writes=[Buu[j]])

            prev = None
            for j in range(KF + 1):
                if i + 1 < n:
                    npipe.spread(j, i + 1, tiles[i + 1], SS, per=2, start=9)
                cur = None
                if j < KF:
                    cur = (j, ffn_p1(j))
                if prev is not None:
                    ffn_p2(*prev)
                prev = cur
            yy, Byy = yo[0], Byo[0]
            for nn in range(KD):
                bo = 6 + (nn % 2)
                mm_group(psum[:, bo, 0:T], PB[bo],
                         [(wdn[:, j, nn * 128:(nn + 1) * 128], uu[:, j, :], [Bwdn, Buu[j]]) for j in range(KF)])
                c.I("dve", "scalar_tensor_tensor", dict(out=yy[:, nn, :], in0=psum[:, bo, 0:T], scalar=mod_ap(l, 5, nn, t["tok"]),
                                                                      in1=hb[:, nn, 1:T + 1], op0=ALU.mult, op1=ALU.add),
                     reads=[PB[bo], Bh, Bmod], writes=[Byy])
            c.dma(dst[t["kind"]][:, t["g0"]:t["g0"] + T].rearrange("(k p) n -> p k n", p=128), yy[:], reads=[Byy])
            if i + 2 < n:
                npipe.load(i + 2, tiles[i + 2])
        c.barrier()
        ar.reset(m0)


    TQ = 512

    def stage_qkv(l, kind, src_lat, src_ctx, need_qctx):
        c.stage = "L%d_qkv" % l
        m0 = ar.mark()
        p = "l%d_" % l
        if kind == 1:
            wname, NQC, NKC, VC = "wa_qkv", 8, 2, 256
        else:
            wname, NQC, NKC, VC = "da_qkv", 8, 8, 1024
        NCOL = (NQC + NKC) * 128 + VC
        VOFF = (NQC + NKC) * 128
        wq = ar.alloc("wqkv", [128, KD, NCOL], BF16)
        Bwq = Buf("wqkv")
        m1 = ar.mark()
        stg = [ar.alloc("stg%d" % i, [128, 2048], F32) for i in range(4)]
        Bstg = bufs(4, "stg")
        load_weight(wq, Bwq, W[p + wname], KD, NCOL, stg, Bstg)
        c.barrier()
        ar.reset(m1)
        cosb = ar.alloc("cosb", [128, SEQ], F32)
        sinb = ar.alloc("sinb", [128, SEQ], F32)
        Btab = Buf("tab")
        c.dma(cosb[:], cos_d, writes=[Btab])
        c.dma(sinb[:], sin_d, writes=[Btab])
        rp32 = ar.alloc("rp32", [128, 128], F32)
        rpb = ar.alloc("rpb", [128, 128], BF16)
        ones2 = ar.alloc("ones2", [128, 128], BF16)
        g8 = ar.alloc("g8", [128, 2], F32)
        Bk = Buf("kconst")
        c.dma(rp32[:], rperm_d, writes=[Bk])
        c.I("dve", "tensor_copy", dict(out=rpb[:], in_=rp32[:]), reads=[Bk], writes=[Bk])
        c.I("pool", "memset", dict(ap=ones2[:], constant=0.0), writes=[Bk])
        c.I("pool", "memset", dict(ap=ones2[0:64, 0:64], constant=1.0), writes=[Bk])
        c.I("pool", "memset", dict(ap=ones2[64:128, 64:128], constant=1.0), writes=[Bk])
        c.I("dve", "tensor_scalar", dict(out=g8[:, 0:1], in0=cvc(p + "qg"), scalar1=8.0, scalar2=None, op0=ALU.mult), reads=[Bcv], writes=[Bk])
        c.I("dve", "tensor_scalar", dict(out=g8[:, 1:2], in0=cvc(p + "kg"), scalar1=8.0, scalar2=None, op0=ALU.mult), reads=[Bcv], writes=[Bk])
        npipe = NormPipe(l, 0, src_lat, src_ctx, nh=2, nx=2, Wt=TQ, halo=False)
        sqc = [ar.alloc("sqc%d" % i, [128, TQ], BF16) for i in range(3)]
        Bsqc = bufs(3, "sqc")
        rsc = [ar.alloc("rsc%d" % i, [128, TQ], F32) for i in range(2)]
        Brsc = bufs(2, "rsc")
        qn = [ar.alloc("qn%d" % i, [128, TQ], BF16) for i in range(4)]
        Bqn = bufs(4, "qn")
        t1 = [ar.alloc("t1_%d" % i, [128, TQ], F32) for i in range(2)]
        Bt1 = bufs(2, "t1")
        t2 = [ar.alloc("t2_%d" % i, [128, TQ], F32) for i in range(2)]
        Bt2 = bufs(2, "t2")
        qr = [ar.alloc("qr%d" % i, [128, TQ], BF16) for i in range(3)]
        Bqr = bufs(3, "qr")
        vt = [ar.alloc("vt%d" % i, [128, VC], BF16) for i in range(2)]
        Bvt = bufs(2, "vt")
        tiles = []
        for b in range(NB):
            for i in range(SEQ // TQ):
                tiles.append(dict(kind="lat", tok=b, g0=b * SEQ + i * TQ, pos=i * TQ))
        tiles.append(dict(kind="ctx", tok=2, g0=0, pos=0))
        n = len(tiles)
        SS = 0
        npipe.load(0, tiles[0])
        npipe.load(1, tiles[1])
        npipe.part1(0, tiles[0], SS)
        npipe.part2(0, tiles[0])
        items = []
        for i, t in enumerate(tiles):
            lat = t["kind"] == "lat"
            chunks = [("k", ci) for ci in range(NKC)]
            if lat or need_qctx:
                chunks += [("q", ci) for ci in range(NQC)]
            for cidx, (qk, ci) in enumerate(chunks):
                items.append(dict(i=i, t=t, lat=lat, cidx=cidx, qk=qk, ci=ci, lastc=(cidx == len(chunks) - 1)))
        nv = [0]

        def phA(m):
            it_ = items[m]
            x, Bx = npipe.xnbuf(it_["i"])
            bq = 1 + m % 3
            col0 = (it_["ci"] if it_["qk"] == "q" else NQC + it_["ci"]) * 128
            mm_group(psum[:, bq, 0:TQ], PB[bq],
                     [(wq[:, k, col0:col0 + 128], x[:, k, :], [Bwq, Bx[k]]) for k in range(KD)])
            c.I("act", "activation", dict(out=sqc[m % 3][:], in_=psum[:, bq, 0:TQ], func=AF.Square),
                reads=[PB[bq]], writes=[Bsqc[m % 3]])

        def phB(m):
            it_ = items[m]
            bq = 1 + m % 3
            bs = 4 + m % 2
            mm_group(psum[:, bs, 0:TQ], PB[bs], [(ones2[:], sqc[m % 3][:], [Bk, Bsqc[m % 3]])])
            c.I("act", "activation", dict(out=rsc[m % 2][:], in_=psum[:, bs, 0:TQ], func=AF.Ln, bias=EPS * HD, scale=1.0),
                reads=[PB[bs]], writes=[Brsc[m % 2]])
            c.I("act", "activation", dict(out=rsc[m % 2][:], in_=rsc[m % 2][:], func=AF.Exp, scale=-0.5),
                reads=[Brsc[m % 2]], writes=[Brsc[m % 2]])
            gcol = 0 if it_["qk"] == "q" else 1
            c.I("dve", "scalar_tensor_tensor", dict(out=qn[m % 4][:], in0=psum[:, bq, 0:TQ], scalar=g8[:, gcol:gcol + 1], in1=rsc[m % 2][:],
                                                    op0=ALU.mult, op1=ALU.mult),
                reads=[PB[bq], Brsc[m % 2], Bk], writes=[Bqn[m % 4]])

        def phC(m):
            it_ = items[m]
            t = it_["t"]
            lat = it_["lat"]
            qk, ci = it_["qk"], it_["ci"]
            dst = {"q": (QT_lat if lat else QT_ctx), "k": (KT_lat if lat else KT_ctx)}[qk]
            drow = ci * 128
            q4 = m % 4
            if lat:
                bp = 6
                mm_group(psum[:, bp, 0:TQ], PB[bp], [(rpb[:], qn[q4][:], [Bk, Bqn[q4]])])
                pos = t["pos"]
                c.I("dve", "tensor_tensor", dict(out=t1[m % 2][:], in0=qn[q4][:], in1=cosb[:, pos:pos + TQ], op=ALU.mult),
                    reads=[Bqn[q4], Btab], writes=[Bt1[m % 2]])
                c.I("dve", "tensor_tensor", dict(out=t2[m % 2][:], in0=psum[:, bp, 0:TQ], in1=sinb[:, pos:pos + TQ], op=ALU.mult),
                    reads=[PB[bp], Btab], writes=[Bt2[m % 2]])
                c.I("pool", "tensor_tensor", dict(out=qr[m % 3][:], in0=t1[m % 2][:], in1=t2[m % 2][:], op=ALU.add),
                    reads=[Bt1[m % 2], Bt2[m % 2]], writes=[Bqr[m % 3]])
                c.dma(dst[drow:drow + 128, t["g0"]:t["g0"] + TQ], qr[m % 3][:], reads=[Bqr[m % 3]])
            else:
                c.dma(dst[drow:drow + 128, t["g0"]:t["g0"] + TQ], qn[q4][:], reads=[Bqn[q4]])

        def emitV(i):
            t = tiles[i]
            x, Bx = npipe.xnbuf(i)
            vdst = V_lat if t["kind"] == "lat" else V_ctx
            for sub in range(TQ // 128):
                sv = nv[0] % 2
                nv[0] += 1
                for c0 in range(0, VC, 512):
                    cw_ = min(512, VC - c0)
                    bv = 7
                    mm_group(psum[:, bv, 0:cw_], PB[bv],
                             [(x[:, k, sub * 128:(sub + 1) * 128], wq[:, k, VOFF + c0:VOFF + c0 + cw_], [Bwq, Bx[k]]) for k in range(KD)])
                    c.I("act", "activation", dict(out=vt[sv][:, c0:c0 + cw_], in_=psum[:, bv, 0:cw_], func=AF.Copy),
                        reads=[PB[bv]], writes=[Bvt[sv]])
                r0 = t["g0"] + sub * 128
                c.dma(vdst[r0:r0 + 128, 0:VC], vt[sv][:], reads=[Bvt[sv]])

        M = len(items)
        for m in range(M + 2):
            if m < M:
                it_ = items[m]
                i = it_["i"]
                if it_["cidx"] == 2 and i + 1 < n:
                    npipe.part1(i + 1, tiles[i + 1], SS)
                if it_["cidx"] == 5 and i + 1 < n:
                    npipe.part2(i + 1, tiles[i + 1])
                phA(m)
            if 0 <= m - 1 < M:
                phB(m - 1)
            if 0 <= m - 2 < M:
                phC(m - 2)
            if m < M and items[m]["lastc"]:
                i = items[m]["i"]
                emitV(i)
                if i + 2 < n:
                    npipe.load(i + 2, tiles[i + 2])
        c.barrier()
        ar.reset(m0)

    def stage_diffattn(l):
        c.stage = "L%d_dattn" % l
        m0 = ar.mark()
        p = "l%d_" % l
        lam_init = 0.8 - 0.6 * math.exp(-0.3 * l)
        NKCH = (SEQ + NCTX) // 128
        NQT = SEQ // 512
        Bsc = Buf("dsc")
        prod = ar.alloc("prod", [128, 128], F32)
        sc_ = ar.alloc("dsc", [128, 8], F32)
        o1, _ = cvoff[p + "lq1"]
        o2, _ = cvoff[p + "lk1"]
        o3, _ = cvoff[p + "lq2"]
        o4, _ = cvoff[p + "lk2"]
        c.I("dve", "tensor_tensor", dict(out=prod[:, 0:64], in0=cv[:, o1:o1 + 64], in1=cv[:, o2:o2 + 64], op=ALU.mult), reads=[Bcv], writes=[Bsc])
        c.I("dve", "tensor_tensor", dict(out=prod[:, 64:128], in0=cv[:, o3:o3 + 64], in1=cv[:, o4:o4 + 64], op=ALU.mult), reads=[Bcv], writes=[Bsc])
        c.I("dve", "reduce_sum", dict(out=sc_[:, 0:1], in_=prod[:, 0:64], axis=mybir.AxisListType.X), reads=[Bsc], writes=[Bsc])
        c.I("dve", "reduce_sum", dict(out=sc_[:, 1:2], in_=prod[:, 64:128], axis=mybir.AxisListType.X), reads=[Bsc], writes=[Bsc])
        c.I("act", "activation", dict(out=sc_[:, 2:4], in_=sc_[:, 0:2], func=AF.Exp), reads=[Bsc], writes=[Bsc])
        c.I("dve", "scalar_tensor_tensor", dict(out=sc_[:, 4:5], in0=sc_[:, 3:4], scalar=-lam_init, in1=sc_[:, 2:3], op0=ALU.add, op1=ALU.subtract),
            reads=[Bsc], writes=[Bsc])
        c.I("dve", "tensor_scalar", dict(out=sc_[:, 5:6], in0=cvc(p + "subln"), scalar1=(1.0 - lam_init), scalar2=None, op0=ALU.mult),
            reads=[Bcv], writes=[Bsc])
        ones32 = ar.alloc("ones32", [128, 128], F32)
        c.I("pool", "memset", dict(ap=ones32[:], constant=1.0), writes=[Bsc])
        kth = [ar.alloc("kth%d" % i, [128, SEQ + NCTX], BF16) for i in range(2)]
        vh = [ar.alloc("vh%d" % i, [128, NKCH, 128], BF16) for i in range(2)]
        qth = [ar.alloc("qth%d" % i, [128, SEQ], BF16) for i in range(2)]
        Bkv = bufs(2, "kvq")
        E = [ar.alloc("E%d" % i, [128, 2, 512], BF16) for i in range(3)]
        BE = bufs(3, "E")
        osb = [ar.alloc("osb%d" % i, [128, 512], F32) for i in range(2)]
        lsb = [ar.alloc("lsb%d" % i, [128, 512], F32) for i in range(2)]
        Bol = Buf("ol")
        od = ar.alloc("od", [128, 512], F32)
        sq32 = ar.alloc("sq32", [128, 512], F32)
        rsd = ar.alloc("rsd", [128, 512], F32)
        Bep = Buf("ep")
        ob = [ar.alloc("ob%d" % i, [128, 512], BF16) for i in range(2)]
        Bob = bufs(2, "ob")

        def load_bh(idx, b, h):
            s_ = idx % 2
            r0 = h * 128
            c.dma(kth[s_][:, 0:SEQ], KT_lat[r0:r0 + 128, b * SEQ:(b + 1) * SEQ], writes=[Bkv[s_]])
            c.dma(kth[s_][:, SEQ:SEQ + NCTX], KT_ctx[r0:r0 + 128, b * NCTX:(b + 1) * NCTX], writes=[Bkv[s_]])
            c.dma(vh[s_][:, 0:SEQ // 128, :], V_lat[b * SEQ:(b + 1) * SEQ, r0:r0 + 128].rearrange("(j p) e -> p j e", p=128), writes=[Bkv[s_]])
            c.dma(vh[s_][:, SEQ // 128:NKCH, :], V_ctx[b * NCTX:(b + 1) * NCTX, r0:r0 + 128].rearrange("(j p) e -> p j e", p=128), writes=[Bkv[s_]])
            c.dma(qth[s_][:], QT_lat[r0:r0 + 128, b * SEQ:(b + 1) * SEQ], writes=[Bkv[s_]])

        bhs = [(b, h) for b in range(NB) for h in range(8)]
        load_bh(0, *bhs[0])
        steps = []
        for idx, (b, h) in enumerate(bhs):
            for qt in range(NQT):
                for j in range(NKCH):
                    steps.append((idx, b, h, qt, j))
        nq = [0]

        def emit_S(n):
            idx, b, h, qt, j = steps[n]
            s_ = idx % 2
            q0 = qt * 512
            A = (n % 2) * 2
            es = n % 3
            for cc in range(2):
                c.I("pe", "matmul", dict(out=psum[:, A + cc, :], lhsT=kth[s_][cc * 64:(cc + 1) * 64, j * 128:(j + 1) * 128],
                                         rhs=qth[s_][cc * 64:(cc + 1) * 64, q0:q0 + 512], start=True, stop=True),
                    reads=[Bkv[s_]], writes=[PB[A + cc]], inc=(cc == 1))
            c.I("act", "activation", dict(out=E[es][:], in_=psum[:, A:A + 2, :], func=AF.Exp, scale=0.125),
                reads=[PB[A], PB[A + 1]], writes=[BE[es]])

        sel = ar.alloc("sel", [64, 2, 128], F32)
        c.I("pool", "memset", dict(ap=sel[:], constant=0.0), writes=[Bsc])
        c.I("pool", "memset", dict(ap=sel[0:1, 0, :], constant=1.0), writes=[Bsc])
        c.I("pool", "memset", dict(ap=sel[32:33, 1, :], constant=1.0), writes=[Bsc])
        l64 = ar.alloc("l64", [64, 512], F32)
        Bl64 = Buf("l64")
        DL = (6, 8, 14) if NKCH >= 20 else (2, 3, 5)
        defer = []

        def emit_OL(n):
            idx, b, h, qt, j = steps[n]
            s_ = idx % 2
            es = n % 3
            last = (j == NKCH - 1)
            for cc in range(2):
                c.I("pe", "matmul", dict(out=psum[:, 4 + cc, :], lhsT=vh[s_][:, j, :], rhs=E[es][:, cc, :], start=(j == 0), stop=last),
                    reads=[Bkv[s_], BE[es]], writes=[PB[4 + cc]], inc=last)
            for cc in range(2):
                c.I("pe", "matmul", dict(out=psum[32 * cc:32 * cc + 32, 6, :], lhsT=ones_bf[:, 0:32], rhs=E[es][:, cc, :], start=(j == 0), stop=last),
                    reads=[Bones, BE[es]], writes=[PB[6]], inc=(last and cc == 1))
            if not last:
                return
            q0 = qt * 512
            for cc in range(2):
                c.I("dve", "tensor_copy", dict(out=osb[cc][:], in_=psum[:, 4 + cc, :]), reads=[PB[4 + cc]], writes=[Bol])
            c.I("dve", "tensor_copy", dict(out=l64[:], in_=psum[0:64, 6, :]), reads=[PB[6]], writes=[Bl64])
            c.I("dve", "reciprocal", dict(out=l64[:], in_=l64[:]), reads=[Bl64], writes=[Bl64])

            def ep_a():
                c.I("pe", "matmul", dict(out=psum[:, 7, :], lhsT=sel[:, 0, :], rhs=l64[:], start=True, stop=True),
                    reads=[Bsc, Bl64], writes=[PB[7]], inc=True)
                c.I("dve", "tensor_tensor", dict(out=osb[0][:], in0=osb[0][:], in1=psum[:, 7, :], op=ALU.mult), reads=[Bol, PB[7]], writes=[Bol])

            def ep_b():
                c.I("pe", "matmul", dict(out=psum[:, 7, :], lhsT=sel[:, 1, :], rhs=l64[:], start=True, stop=True),
                    reads=[Bsc, Bl64], writes=[PB[7]], inc=True)
                c.I("dve", "tensor_tensor", dict(out=osb[1][:], in0=osb[1][:], in1=psum[:, 7, :], op=ALU.mult), reads=[Bol, PB[7]], writes=[Bol])
                c.I("dve", "scalar_tensor_tensor", dict(out=od[:], in0=osb[1][:], scalar=sc_[:, 4:5], in1=osb[0][:], op0=ALU.mult, op1=ALU.add),
                    reads=[Bol, Bsc], writes=[Bep])
                c.I("pool", "tensor_tensor", dict(out=sq32[:], in0=od[:], in1=od[:], op=ALU.mult), reads=[Bep], writes=[Bep])

            def ep_c():
                c.I("pe", "matmul", dict(out=psum[:, 7, :], lhsT=ones32[:], rhs=sq32[:], start=True, stop=True),
                    reads=[Bsc, Bep], writes=[PB[7]], inc=True)
                c.I("act", "activation", dict(out=rsd[:], in_=psum[:, 7, :], func=AF.Ln, bias=EPS, scale=1.0 / 128.0),
                    reads=[PB[7]], writes=[Bep])
                c.I("act", "activation", dict(out=rsd[:], in_=rsd[:], func=AF.Exp, scale=-0.5), reads=[Bep], writes=[Bep])
                so = nq[0] % 2
                nq[0] += 1
                c.I("dve", "scalar_tensor_tensor", dict(out=ob[so][:], in0=od[:], scalar=sc_[:, 5:6], in1=rsd[:], op0=ALU.mult, op1=ALU.mult),
                    reads=[Bep, Bsc], writes=[Bob[so]])
                c.dma(AT_lat[h * 128:(h + 1) * 128, b * SEQ + q0:b * SEQ + q0 + 512], ob[so][:], reads=[Bob[so]])
            defer.append((n + 1 + DL[0], ep_a))
            defer.append((n + 1 + DL[1], ep_b))
            defer.append((n + 1 + DL[2], ep_c))

        N = len(steps)
        for n in range(N):
            emit_S(n)
            while defer and defer[0][0] <= n:
                defer.pop(0)[1]()
            if n >= 1:
                emit_OL(n - 1)
            idx, _, _, qt, j = steps[n]
            if qt == 0 and j == 0 and idx + 1 < len(bhs):
                load_bh(idx + 1, *bhs[idx + 1])
        emit_OL(N - 1)
        while defer:
            defer.pop(0)[1]()
        c.barrier()
        ar.reset(m0)

    def stage_winattn(l, ctx_out):
        c.stage = "L%d_wattn" % l
        m0 = ar.mark()
        p = "l%d_" % l
        NKCH = (SEQ + NCTX) // 128
        NQB = SEQ // 128
        Bsc = Buf("wsc")
        msk32 = ar.alloc("msk32", [128, 1024], F32)
        mskb = ar.alloc("mskb", [128, 2, 512], BF16)
        c.dma(msk32[:], mask_d, writes=[Bsc])
        c.I("dve", "tensor_copy", dict(out=mskb[:].rearrange("p a n -> p (a n)"), in_=msk32[:]), reads=[Bsc], writes=[Bsc])
        esk = ar.alloc("esk", [128, 16], F32)
        c.I("act", "activation", dict(out=esk[:], in_=cvc(p + "sink", 0, 16), func=AF.Exp), reads=[Bcv], writes=[Bsc])
        zer = ar.alloc("zer", [64, 128], F32)
        c.I("pool", "memset", dict(ap=zer[:], constant=0.0), writes=[Bsc])
        ES = ar.alloc("ES", [64, 4, 4, 128], F32)
        for hk in range(4):
            for g in range(4):
                c.I("dve", "tensor_scalar", dict(out=ES[:, hk, g, :], in0=zer[:], scalar1=esk[0:64, hk * 4 + g:hk * 4 + g + 1], scalar2=None, op0=ALU.add),
                    reads=[Bsc], writes=[Bsc])
        kth = [ar.alloc("kth%d" % i, [128, SEQ + NCTX], BF16) for i in range(2)]
        vh = [ar.alloc("vh%d" % i, [128, NKCH, 128], BF16) for i in range(2)]
        qg = [ar.alloc("qg%d" % i, [128, 4, SEQ + NCTX], BF16) for i in range(2)]
        Bkv = bufs(2, "kvq")
        for i_ in range(2):
            c.I("pool", "memset", dict(ap=kth[i_][64:128, :], constant=0.0), writes=[Bkv[i_]])
            c.I("pool", "memset", dict(ap=qg[i_][64:128, :, :], constant=0.0), writes=[Bkv[i_]])
            c.I("pool", "memset", dict(ap=vh[i_][:, :, 64:128], constant=1.0), writes=[Bkv[i_]])
        E = [ar.alloc("E%d" % i, [128, 512], BF16) for i in range(4)]
        BE = bufs(4, "E")
        lt = [ar.alloc("lt%d" % i, [64, 512], F32) for i in range(2)]
        Blt = bufs(2, "lt")
        ob = [ar.alloc("ob%d" % i, [64, 4, 128], BF16) for i in range(3)]
        Bob = bufs(3, "ob")

        def load_bh(idx, b, hk):
            s_ = idx % 2
            c.dma(kth[s_][0:64, 0:SEQ], KT_lat[hk * 64:(hk + 1) * 64, b * SEQ:(b + 1) * SEQ], writes=[Bkv[s_]])
            c.dma(kth[s_][0:64, SEQ:SEQ + NCTX], KT_ctx[hk * 64:(hk + 1) * 64, b * NCTX:(b + 1) * NCTX], writes=[Bkv[s_]])
            c.dma(vh[s_][:, 0:SEQ // 128, 0:64], V_lat[b * SEQ:(b + 1) * SEQ, hk * 64:(hk + 1) * 64].rearrange("(j p) e -> p j e", p=128), writes=[Bkv[s_]])
            c.dma(vh[s_][:, SEQ // 128:NKCH, 0:64], V_ctx[b * NCTX:(b + 1) * NCTX, hk * 64:(hk + 1) * 64].rearrange("(j p) e -> p j e", p=128), writes=[Bkv[s_]])
            c.dma(qg[s_][0:64, :, 0:SEQ], QT_lat[hk * 256:(hk + 1) * 256, b * SEQ:(b + 1) * SEQ].rearrange("(g d) n -> d g n", d=64), writes=[Bkv[s_]])
            if ctx_out:
                c.dma(qg[s_][0:64, :, SEQ:SEQ + NCTX], QT_ctx[hk * 256:(hk + 1) * 256, b * NCTX:(b + 1) * NCTX].rearrange("(g d) n -> d g n", d=64), writes=[Bkv[s_]])

        bhs = [(b, hk) for b in range(NB) for hk in range(4)]
        load_bh(0, *bhs[0])
        steps = []
        nblk = 0
        for idx, (b, hk) in enumerate(bhs):
            blocks = [("lat", qb) for qb in range(NQB)]
            if ctx_out:
                blocks += [("ctx", qb) for qb in range(NCTX // 128)]
            for (bk, qb) in blocks:
                if bk == "lat":
                    qc0 = qb * 128
                    kcs = []
                    if qb > 0:
                        kcs.append((qb - 1, 0))
                    kcs.append((qb, None))
                    if qb < NQB - 1:
                        kcs.append((qb + 1, 1))
                    kcs += [(NQB, None), (NQB + 1, None)]
                else:
                    qc0 = SEQ + qb * 128
                    kcs = [(NQB, None), (NQB + 1, None)]
                for ki, (kc, mk) in enumerate(kcs):
                    steps.append(dict(idx=idx, b=b, hk=hk, bk=bk, qb=qb, qc0=qc0, kc=kc, mk=mk, first=(ki == 0),
                                      last=(ki == len(kcs) - 1), ol=nblk % 2, nblk=nblk, newidx=(ki == 0 and (bk, qb) == blocks[0])))
                nblk += 1

        def emit_S(n):
            st = steps[n]
            s_ = st["idx"] % 2
            sb_ = n % 4
            es = n % 4
            kc, qc0 = st["kc"], st["qc0"]
            c.I("pe", "matmul", dict(out=psum[:, sb_, :], lhsT=kth[s_][:, kc * 128:(kc + 1) * 128], rhs=qg[s_][:, :, qc0:qc0 + 128],
                                     start=True, stop=True), reads=[Bkv[s_]], writes=[PB[sb_]], inc=True)
            c.I("act", "activation", dict(out=E[es][:], in_=psum[:, sb_, :], func=AF.Exp, scale=0.125), reads=[PB[sb_]], writes=[BE[es]])
            if st["mk"] is not None:
                c.I("pool", "tensor_tensor", dict(out=E[es][:], in0=E[es][:], in1=mskb[:, st["mk"], :], op=ALU.mult), reads=[BE[es], Bsc], writes=[BE[es]])

        def emit_OL(n):
            st = steps[n]
            s_ = st["idx"] % 2
            es = n % 4
            ol = st["ol"]
            bo_ = 4 + ol
            kc, hk, b, qb = st["kc"], st["hk"], st["b"], st["qb"]
            c.I("pe", "matmul", dict(out=psum[:, bo_, :], lhsT=vh[s_][:, kc, :], rhs=E[es][:], start=st["first"], stop=st["last"]),
                reads=[Bkv[s_], BE[es]], writes=[PB[bo_]], inc=st["last"])
            if not st["last"]:
                return
            c.I("dve", "tensor_tensor", dict(out=lt[ol][:], in0=psum[64:128, bo_, :], in1=ES[:, hk, :, :].rearrange("p g q -> p (g q)"), op=ALU.add),
                reads=[PB[bo_], Bsc], writes=[Blt[ol]])
            c.I("act", "activation", dict(out=lt[ol][:], in_=lt[ol][:], func=AF.Ln), reads=[Blt[ol]], writes=[Blt[ol]])
            c.I("act", "activation", dict(out=lt[ol][:], in_=lt[ol][:], func=AF.Exp, scale=-1.0), reads=[Blt[ol]], writes=[Blt[ol]])
            so = st["nblk"] % 3
            c.I("dve", "tensor_tensor", dict(out=ob[so][:].rearrange("p g q -> p (g q)"), in0=psum[0:64, bo_, :], in1=lt[ol][:], op=ALU.mult),
                reads=[PB[bo_], Blt[ol]], writes=[Bob[so]])
            if st["bk"] == "lat":
                dst = AT_lat[hk * 256:(hk + 1) * 256, b * SEQ + qb * 128:b * SEQ + (qb + 1) * 128]
            else:
                dst = AT_ctx[hk * 256:(hk + 1) * 256, b * NCTX + qb * 128:b * NCTX + (qb + 1) * 128]
            c.dma(dst.rearrange("(g d) n -> d g n", d=64), ob[so][:], reads=[Bob[so]])

        N = len(steps)
        pend_load = None
        for n in range(N):
            emit_S(n)
            if n >= 2:
                emit_OL(n - 2)
                if pend_load is not None and pend_load[0] <= n - 2:
                    load_bh(pend_load[1], *bhs[pend_load[1]])
                    pend_load = None
            st = steps[n]
            if st["newidx"] and st["idx"] + 1 < len(bhs):
                pend_load = (n - 1, st["idx"] + 1)
                if n == 0:
                    load_bh(1, *bhs[1])
                    pend_load = None
        emit_OL(N - 2)
        emit_OL(N - 1)
        c.barrier()
        ar.reset(m0)

    def stage_outproj(l, wname, src_lat, src_ctx, dst_lat, dst_ctx, with_ctx):
        c.stage = "L%d_oproj" % l
        m0 = ar.mark()
        p = "l%d_" % l
        wo = ar.alloc("wo", [128, KD, D], BF16)
        Bwo = Buf("wo")
        m1 = ar.mark()
        stg = [ar.alloc("stg%d" % i, [128, 2048], F32) for i in range(4)]
        Bstg = bufs(4, "stg")
        load_weight(wo, Bwo, W[p + wname], KD, D, stg, Bstg)
        c.barrier()
        ar.reset(m1)
        at = [ar.alloc("at%d" % i, [128, KD, TQ], BF16) for i in range(2)]
        Bat = bufs(2, "at")
        hh = [ar.alloc("hh%d" % i, [128, KD, TQ], F32) for i in range(2)]
        Bhh = bufs(2, "hh")
        yo = [ar.alloc("yo%d" % i, [128, KD, TQ], F32) for i in range(2)]
        Byo = bufs(2, "yo")
        tiles = []
        for b in range(NB):
            for i in range(SEQ // TQ):
                tiles.append(dict(kind="lat", tok=b, g0=b * SEQ + i * TQ))
        if with_ctx:
            tiles.append(dict(kind="ctx", tok=2, g0=0))
        srcA = {"lat": AT_lat, "ctx": AT_ctx}
        srcH = {"lat": src_lat, "ctx": src_ctx}
        dstH = {"lat": dst_lat, "ctx": dst_ctx}

        def load(i):
            t = tiles[i]
            g0 = t["g0"]
            c.dma(at[i % 2][:], srcA[t["kind"]][:, g0:g0 + TQ].rearrange("(k p) n -> p k n", p=128), writes=[Bat[i % 2]])
            c.dma(hh[i % 2][:], srcH[t["kind"]][:, g0:g0 + TQ].rearrange("(k p) n -> p k n", p=128), writes=[Bhh[i % 2]])
        load(0)
        for i, t in enumerate(tiles):
            if i + 1 < len(tiles):
                load(i + 1)
            for nn in range(KD):
                bo = nn % 8
                mm_group(psum[:, bo, 0:TQ], PB[bo],
                         [(wo[:, m, nn * 128:(nn + 1) * 128], at[i % 2][:, m, :], [Bwo, Bat[i % 2]]) for m in range(KD)])
                c.I("dve", "scalar_tensor_tensor", dict(out=yo[i % 2][:, nn, :], in0=psum[:, bo, 0:TQ], scalar=mod_ap(l, 2, nn, t["tok"]),
                                                        in1=hh[i % 2][:, nn, :], op0=ALU.mult, op1=ALU.add),
                    reads=[PB[bo], Bhh[i % 2], Bmod], writes=[Byo[i % 2]])
            g0 = t["g0"]
            c.dma(dstH[t["kind"]][:, g0:g0 + TQ].rearrange("(k p) n -> p k n", p=128), yo[i % 2][:], reads=[Byo[i % 2]])
        c.barrier()
        ar.reset(m0)

    stage_ada()
    cur_lat, cur_ctx = xT, ctxT
    for li, l in enumerate(layers if stop != "ada" else ()):
        kind = _layer_kind(l)
        ctx_after = any(j % 3 != 0 for j in range(l + 1, 4))
        lastl = (li == len(layers) - 1)
        mid_lat, mid_ctx = hbuf[0], hcbuf[0]
        out_lat = yT if lastl else hbuf[1]
        out_ctx = (ycT if debug else hcbuf[1]) if lastl else hcbuf[1]
        if kind == 0:
            stage_sconv(l, cur_lat, cur_ctx, (out_lat if stop == "mix" else mid_lat), (out_ctx if stop == "mix" else mid_ctx), ctx_after)
        elif kind == 1:
            stage_qkv(l, 1, cur_lat, cur_ctx, ctx_after)
            if stop != "qkv":
                stage_winattn(l, ctx_after)
                stage_outproj(l, "wa_out", cur_lat, cur_ctx, (out_lat if stop == "mix" else mid_lat), (out_ctx if stop == "mix" else mid_ctx), ctx_after)
        else:
            stage_qkv(l, 2, cur_lat, cur_ctx, ctx_after)
            if stop != "qkv":
                stage_diffattn(l)
                stage_outproj(l, "da_out", cur_lat, cur_ctx, (out_lat if stop == "mix" else mid_lat), (out_ctx if stop == "mix" else mid_ctx), ctx_after)
        if stop in ("mix", "qkv"):
            break
        stage_ffn(l, mid_lat, mid_ctx, out_lat, out_ctx, ctx_after)
        cur_lat, cur_ctx = out_lat, out_ctx
    c.emit()
    nc._inst_stage = c.inst_stage
    return nc


def _prep_inputs(inp, core):
    b0 = core * NB
    m = {}
    x = inp["x"][b0:b0 + NB]
    m["xT"] = np.ascontiguousarray(x.reshape(NB * SEQ, D).T)
    cx = inp["ctx"][b0:b0 + NB]
    m["ctxT"] = np.ascontiguousarray(cx.reshape(NB * NCTX, D).T)
    c3 = np.zeros((4, D), np.float32)
    c3[0:NB] = inp["c"][b0:b0 + NB]
    c3[2] = inp["c_ctx"]
    m["c3T"] = np.ascontiguousarray(c3.T)
    return m


def _run(inp, layers=(0, 1, 2, 3), debug=False, cores=N_CORES, stop=None, trace=False):
    nc = build_program(layers=layers, debug=debug, stop=stop)
    cvec = _pack_cvec(inp)
    cosT, sinT, rperm = _rope_tables()
    shared = {"cvec": cvec, "cosT": cosT, "sinT": sinT, "rperm": rperm, "masks": _band_masks()}
    for l in range(4):
        p = "l%d_" % l
        names = ["ada_w", "ffn_up", "ffn_down"]
        k = _layer_kind(l)
        names += {0: ["sc_in", "sc_out"], 1: ["wa_qkv", "wa_out"], 2: ["da_qkv", "da_out"]}[k]
        for nme in names:
            shared[p + nme] = np.ascontiguousarray(inp[p + nme], dtype=np.float32)
    in_maps = []
    for core in range(cores):
        m = dict(shared)
        m.update(_prep_inputs(inp, core))
        in_maps.append(m)
    res = run_bass_kernel_spmd(nc, in_maps, core_ids=list(range(cores)), trace=trace)
    res._inst_stage = getattr(nc, "_inst_stage", {})
    return res


_INPUT_NAMES = (
    "x", "c", "ctx", "c_ctx",
    "l0_ada_w", "l0_ada_b", "l0_norm1", "l0_norm2", "l0_sc_in", "l0_sc_conv", "l0_sc_out",
    "l0_ffn_up", "l0_ffn_conv_w", "l0_ffn_conv_b", "l0_ffn_down",
    "l1_ada_w", "l1_ada_b", "l1_norm1", "l1_norm2", "l1_wa_qkv", "l1_wa_qnorm", "l1_wa_knorm", "l1_wa_sink", "l1_wa_out",
    "l1_ffn_up", "l1_ffn_conv_w", "l1_ffn_conv_b", "l1_ffn_down",
    "l2_ada_w", "l2_ada_b", "l2_norm1", "l2_norm2", "l2_da_qkv", "l2_da_qnorm", "l2_da_knorm",
    "l2_da_lq1", "l2_da_lk1", "l2_da_lq2", "l2_da_lk2", "l2_da_subln", "l2_da_out",
    "l2_ffn_up", "l2_ffn_conv_w", "l2_ffn_conv_b", "l2_ffn_down",
    "l3_ada_w", "l3_ada_b", "l3_norm1", "l3_norm2", "l3_sc_in", "l3_sc_conv", "l3_sc_out",
    "l3_ffn_up", "l3_ffn_conv_w", "l3_ffn_conv_b", "l3_ffn_down",
)


def kernel(**inputs):
    inp = {k: np.asarray(inputs[k]) for k in _INPUT_NAMES}
    res = _run(inp)
    out = np.empty((N_CORES * NB, SEQ, D), np.float32)
    for core in range(N_CORES):
        yT = res.results[core]["yT"]
        out[core * NB:(core + 1) * NB] = yT.T.reshape(NB, SEQ, D)
    return out
```

```python
import math
import numpy as np
import concourse.bass as bass
import concourse.mybir as mybir
from concourse.bass_utils import run_bass_kernel_spmd

F32 = mybir.dt.float32
BF16 = mybir.dt.bfloat16
ALU = mybir.AluOpType
AF = mybir.ActivationFunctionType

D = 1024
KD = 8
SEQ = 4096
NB = 2
NCTX = 256
DFF = 2816
KF = 22
HD = 64
EPS = 1e-6
N_CORES = 8
T = 256
ENGS = ("pe", "act", "dve", "pool", "sp")
SEM_LIMIT = 30000


class Buf:
    __slots__ = ("name", "w", "r")

    def __init__(self, name=""):
        self.name = name
        self.w = None
        self.r = {}


def bufs(n, name=""):
    return [Buf("%s%d" % (name, i)) for i in range(n)]


class Ctx:
    N_DMA_SEMS = 32

    def __init__(self, nc):
        self.nc = nc
        self.epoch = {e: 0 for e in ENGS}
        self.semobj = {}
        self.cnt = {e: 0 for e in ENGS}
        self.pending = {e: False for e in ENGS}
        self.waited = {e: {} for e in ENGS}
        self.ops = {e: [] for e in ENGS}
        for e in ENGS:
            self.semobj[("e", e, 0)] = nc.alloc_semaphore("prog_%s_0" % e)
        self.dsem = [nc.alloc_semaphore("dma%d" % i) for i in range(self.N_DMA_SEMS)]
        self.dval = [0] * self.N_DMA_SEMS
        self.drr = 0
        for i, s in enumerate(self.dsem):
            self.semobj[("d", i)] = s
        self.n_ops = 0
        self.stage = "init"
        self.inst_stage = {}

    def _key(self, e):
        return ("e", e, self.epoch[e])

    def _deps(self, eng, reads, writes, same_engine_sync):
        deps = {}

        def add(t):
            if t is None:
                return
            k, v = t
            if deps.get(k, 0) < v:
                deps[k] = v
        for b in reads:
            add(b.w)
        for b in writes:
            add(b.w)
            for t in b.r.values():
                add(t)
        waits = []
        mykey = self._key(eng)
        for k, v in deps.items():
            if k == mykey:
                if not same_engine_sync or v > self.cnt[eng]:
                    continue
            if self.waited[eng].get(k, 0) >= v:
                continue
            self.waited[eng][k] = v
            waits.append((self.semobj[k], v))
        return waits

    def _mark(self, ticket, reads, writes):
        k = ticket[0]
        for b in reads:
            b.r[k] = ticket
        for b in writes:
            b.w = ticket
            b.r = {}

    def op(self, eng, fn, reads=(), writes=(), inc=True, sync_same=None):
        if sync_same is None:
            sync_same = (eng != "pe")
        if inc and self.cnt[eng] >= SEM_LIMIT and not self.pending[eng]:
            self.epoch[eng] += 1
            self.cnt[eng] = 0
            self.semobj[self._key(eng)] = self.nc.alloc_semaphore(
                "prog_%s_%d" % (eng, self.epoch[eng]))
        waits = self._deps(eng, reads, writes, sync_same)
        key = self._key(eng)
        if inc:
            self.cnt[eng] += 1
            ticket = (key, self.cnt[eng])
            self.pending[eng] = False
            incspec = (self.semobj[key], 1)
        else:
            ticket = (key, self.cnt[eng] + 1)
            self.pending[eng] = True
            incspec = None
        self.ops[eng].append((waits, fn, incspec, self.stage))
        self._mark(ticket, reads, writes)
        self.n_ops += 1
        return ticket

    def I(self, eng, meth, kw, reads=(), writes=(), inc=True, sync_same=None):
        def fn(h, meth=meth, kw=kw):
            return getattr(h, meth)(**kw)
        return self.op(eng, fn, reads=reads, writes=writes, inc=inc, sync_same=sync_same)

    def dma(self, out_ap, in_ap, reads=(), writes=(), q="sp"):
        i = self.drr
        self.drr = (self.drr + 1) % self.N_DMA_SEMS
        waits = self._deps(q, reads, writes, True)
        k = ("d", i)
        if self.dval[i] > 0 and self.waited[q].get(k, 0) < self.dval[i]:
            self.waited[q][k] = self.dval[i]
            waits.append((self.dsem[i], self.dval[i]))
        self.dval[i] += 16
        ticket = (k, self.dval[i])

        def fn(h, out_ap=out_ap, in_ap=in_ap):
            return h.dma_start(out=out_ap, in_=in_ap)
        self.ops[q].append((waits, fn, (self.dsem[i], 16), self.stage))
        self._mark(ticket, reads, writes)
        self.n_ops += 1
        return ticket

    def wait_all(self, eng):
        waits = []
        for (k, s) in list(self.semobj.items()):
            if k[0] == "e":
                e = k[1]
                if e == eng:
                    continue
                if k[2] == self.epoch[e]:
                    assert not self.pending[e], "pending non-inc op on " + e
                    v = self.cnt[e]
                else:
                    v = SEM_LIMIT
            else:
                v = self.dval[k[1]]
            if v > 0 and self.waited[eng].get(k, 0) < v:
                self.waited[eng][k] = v
                waits.append((s, v))
        if waits:
            self.ops[eng].append((waits, None, None, self.stage))

    def barrier(self):
        for e in ENGS:
            self.wait_all(e)

    def emit(self):
        nc = self.nc
        self.wait_all("sp")
        h = {"pe": nc.tensor, "act": nc.scalar, "dve": nc.vector, "pool": nc.gpsimd,
             "sp": nc.sync}
        with nc.Block() as block:
            def run(e):
                def body(hh):
                    for waits, fn, incspec, stg in self.ops[e]:
                        for s, v in waits:
                            hh.wait_ge(s, v)
                        if fn is None:
                            continue
                        ins = fn(hh)
                        try:
                            self.inst_stage[ins.ins.name] = stg
                        except Exception:
                            pass
                        if incspec is not None:
                            ins.then_inc(incspec[0], incspec[1])
                return body
            block.tensor(run("pe"))
            block.scalar(run("act"))
            block.vector(run("dve"))
            block.gpsimd(run("pool"))
            block.sync(run("sp"))


class Arena:
    LO = 16896
    HI = 229376

    def __init__(self, nc):
        self.nc = nc
        self.top = self.LO
        self.n = 0

    def alloc(self, name, shape, dtype):
        nbytes = int(np.prod(shape[1:])) * (2 if dtype == BF16 else 4)
        off = (self.top + 63) // 64 * 64
        assert off + nbytes <= self.HI, "SBUF overflow at %s: %d" % (name, off + nbytes)
        self.top = off + nbytes
        self.n += 1
        return self.nc.alloc_sbuf_tensor_at("%s_%d" % (name, self.n), list(shape), dtype, offset=off)

    def mark(self):
        return self.top

    def reset(self, m):
        self.top = m


def _layer_kind(l):
    return l % 3


def _cvec_layout():
    off = {}
    n = 0

    def add(name, w):
        nonlocal n
        off[name] = (n, w)
        n += w
    for l in range(4):
        p = "l%d_" % l
        add(p + "ada_b", 48)
        add(p + "norm1", 8)
        add(p + "norm2", 8)
        add(p + "ffn_conv_w", 3 * KF)
        add(p + "ffn_conv_b", KF)
        k = _layer_kind(l)
        if k == 0:
            add(p + "sc_conv", 3 * KD)
        elif k == 1:
            add(p + "qg", 1)
            add(p + "kg", 1)
            add(p + "sink", 16)
        else:
            add(p + "qg", 1)
            add(p + "kg", 1)
            for v in ("lq1", "lk1", "lq2", "lk2"):
                add(p + v, 64)
            add(p + "subln", 1)
    return off, n


def _pk(v, k):
    return np.ascontiguousarray(v.reshape(k, 128).T)


def _pack_cvec(inp):
    off, n = _cvec_layout()
    cv = np.zeros((128, n), np.float32)

    def put(name, arr):
        o, w = off[name]
        assert arr.shape == (128, w), (name, arr.shape, w)
        cv[:, o:o + w] = arr
    for l in range(4):
        p = "l%d_" % l
        put(p + "ada_b", _pk(inp[p + "ada_b"], 48))
        put(p + "norm1", _pk(inp[p + "norm1"], 8))
        put(p + "norm2", _pk(inp[p + "norm2"], 8))
        cw = inp[p + "ffn_conv_w"]
        put(p + "ffn_conv_w", np.concatenate([_pk(cw[t], KF) for t in range(3)], axis=1))
        put(p + "ffn_conv_b", _pk(inp[p + "ffn_conv_b"], KF))
        k = _layer_kind(l)
        if k == 0:
            cw = inp[p + "sc_conv"]
            put(p + "sc_conv", np.concatenate([_pk(cw[t], KD) for t in range(3)], axis=1))
        elif k == 1:
            put(p + "qg", np.tile(inp[p + "wa_qnorm"], 2)[:, None])
            put(p + "kg", np.tile(inp[p + "wa_knorm"], 2)[:, None])
            put(p + "sink", np.broadcast_to(inp[p + "wa_sink"][None, :], (128, 16)))
        else:
            put(p + "qg", np.tile(inp[p + "da_qnorm"], 2)[:, None])
            put(p + "kg", np.tile(inp[p + "da_knorm"], 2)[:, None])
            for v in ("lq1", "lk1", "lq2", "lk2"):
                put(p + v, np.broadcast_to(inp[p + "da_" + v][None, :], (128, 64)))
            put(p + "subln", inp[p + "da_subln"][:, None])
    return cv


def _rope_tables():
    rows = SEQ // 64
    row = np.repeat(np.arange(rows, dtype=np.float32), 64)
    col = np.tile(np.arange(64, dtype=np.float32), rows)
    m = HD // 4
    inv_freq = (np.float32(10000.0) ** (-np.arange(m, dtype=np.float32) / np.float32(m))).astype(np.float32)
    cos = np.zeros((128, SEQ), np.float32)
    sin = np.zeros((128, SEQ), np.float32)
    for p in range(128):
        loc = p % 64
        axis = loc // 32
        mm = loc % 16
        ang = (row if axis == 0 else col) * inv_freq[mm]
        cos[p] = np.cos(ang)
        sin[p] = np.sin(ang)
    R = np.zeros((128, 128), np.float32)
    for dst in range(128):
        loc = dst % 64
        half = (loc % 32) // 16
        if half == 0:
            R[dst + 16, dst] = -1.0
        else:
            R[dst - 16, dst] = 1.0
    return cos, sin, R


def _band_masks():
    k = np.arange(128)[:, None]
    q = np.arange(128)[None, :]
    prev = (k >= q).astype(np.float32)
    nxt = (k <= q).astype(np.float32)
    return np.ascontiguousarray(np.concatenate([np.tile(prev, (1, 4)), np.tile(nxt, (1, 4))], axis=1))


def build_program(layers=(0, 1, 2, 3), debug=False, stop=None):
    nc = bass.Bass("TRN2", target_bir_lowering=False)
    c = Ctx(nc)
    ar = Arena(nc)
    NT = NB * SEQ
    NC_TOK = NB * NCTX
    cvoff, ncv = _cvec_layout()

    def din(name, shape, dt=F32):
        return nc.dram_tensor(name, list(shape), dt, kind="ExternalInput").ap()

    xT = din("xT", [D, NT])
    ctxT = din("ctxT", [D, NC_TOK])
    c3T = din("c3T", [D, 4])
    cvec_d = din("cvec", [128, ncv])
    W = {}
    for l in range(4):
        p = "l%d_" % l
        W[p + "ada_w"] = din(p + "ada_w", [D, 6 * D])
        k = _layer_kind(l)
        if k == 0:
            W[p + "sc_in"] = din(p + "sc_in", [D, 3 * D])
            W[p + "sc_out"] = din(p + "sc_out", [D, D])
        elif k == 1:
            W[p + "wa_qkv"] = din(p + "wa_qkv", [D, 1536])
            W[p + "wa_out"] = din(p + "wa_out", [D, D])
        else:
            W[p + "da_qkv"] = din(p + "da_qkv", [D, 3 * D])
            W[p + "da_out"] = din(p + "da_out", [D, D])
        W[p + "ffn_up"] = din(p + "ffn_up", [D, 2 * DFF])
        W[p + "ffn_down"] = din(p + "ffn_down", [DFF, D])
    cos_d = din("cosT", [128, SEQ])
    sin_d = din("sinT", [128, SEQ])
    rperm_d = din("rperm", [128, 128])
    mask_d = din("masks", [128, 2 * 512])
    QT_lat = nc.dram_tensor("QT_lat", [D, NT], BF16).ap()
    QT_ctx = nc.dram_tensor("QT_ctx", [D, NC_TOK], BF16).ap()
    KT_lat = nc.dram_tensor("KT_lat", [D, NT], BF16).ap()
    KT_ctx = nc.dram_tensor("KT_ctx", [D, NC_TOK], BF16).ap()
    V_lat = nc.dram_tensor("V_lat", [NT, D], BF16).ap()
    V_ctx = nc.dram_tensor("V_ctx", [NC_TOK, D], BF16).ap()
    AT_lat = nc.dram_tensor("AT_lat", [D, NT], BF16).ap()
    AT_ctx = nc.dram_tensor("AT_ctx", [D, NC_TOK], BF16).ap()
    yT = nc.dram_tensor("yT", [D, NT], F32, kind="ExternalOutput").ap()
    if debug:
        ycT = nc.dram_tensor("ycT", [D, NC_TOK], F32, kind="ExternalOutput").ap()
        modo = nc.dram_tensor("modo", [128, 4 * 6 * KD * 4], F32, kind="ExternalOutput").ap()
    hbuf = [nc.dram_tensor("hs%d" % i, [D, NT], F32).ap() for i in range(2)]
    hcbuf = [nc.dram_tensor("hcs%d" % i, [D, NC_TOK], F32).ap() for i in range(2)]

    cv = ar.alloc("cv", [128, ncv], F32)
    Bcv = Buf("cv")
    ones_bf = ar.alloc("ones", [128, 128], BF16)
    Bones = Buf("ones")
    mod = ar.alloc("mod", [128, 4 * 6 * KD * 4], F32)
    modA = ar.alloc("modA", [128, 4 * 2 * KD * 4], F32)
    Bmod = Buf("mod")
    psum = nc.alloc_psum_tensor("ps", [128, 8, 512], F32)
    PB = bufs(8, "psb")

    def cvc(name, j=0, w=1):
        o, _ = cvoff[name]
        return cv[:, o + j:o + j + w]

    def mod_ap(l, j, k, tok):
        i = ((l * 6 + j) * KD + k) * 4 + tok
        return mod[:, i:i + 1]

    def modA_ap(l, which, k, tok):
        i = ((l * 2 + which) * KD + k) * 4 + tok
        return modA[:, i:i + 1]

    c.dma(cv[:], cvec_d, writes=[Bcv])
    c.I("pool", "memset", dict(ap=ones_bf[:], constant=1.0), writes=[Bones])
    c.I("pool", "memset", dict(ap=mod[:], constant=0.0), writes=[Bmod])
    c.I("pool", "memset", dict(ap=modA[:], constant=0.0), writes=[Bmod])

    persist_mark = ar.mark()

    cast_rr = [0]

    def cast_op(dst_ap, src_ap, reads, writes):
        e = ("dve", "act", "pool")[cast_rr[0] % 3]
        cast_rr[0] += 1
        if e == "act":
            c.I("act", "activation", dict(out=dst_ap, in_=src_ap, func=AF.Copy), reads=reads, writes=writes)
        else:
            c.I(e, "tensor_copy", dict(out=dst_ap, in_=src_ap), reads=reads, writes=writes)

    def load_weight(dst, Bdst, src, K, N, stg, Bstg, cols=2048):
        i = 0
        for k in range(K):
            for n0 in range(0, N, cols):
                n1 = min(N, n0 + cols)
                s = i % len(stg)
                i += 1
                c.dma(stg[s][:, 0:n1 - n0], src[k * 128:(k + 1) * 128, n0:n1], writes=[Bstg[s]])
                cast_op(dst[:, k, n0:n1], stg[s][:, 0:n1 - n0], [Bstg[s]], [Bdst])

    def mm_group(out_ap, Bout, terms, extra_reads=()):
        n = len(terms)
        for i, (l_ap, r_ap, rb) in enumerate(terms):
            c.I("pe", "matmul", dict(out=out_ap, lhsT=l_ap, rhs=r_ap, start=(i == 0), stop=(i == n - 1)),
                 reads=list(rb) + list(extra_reads), writes=[Bout], inc=(i == n - 1))

    def stage_ada():
        c.stage = "ada"
        m0 = ar.mark()
        sc = ar.alloc("silu_c", [128, KD, 4], F32)
        Bsc = Buf("sc")
        c3 = ar.alloc("c3", [128, KD, 4], F32)
        Bc3 = Buf("c3")
        wst = [ar.alloc("adaw%d" % i, [128, KD, 1024], F32) for i in range(2)]
        Bw = bufs(2, "adaw")
        wsb = [ar.alloc("adawb%d" % i, [128, KD, 1024], BF16) for i in range(2)]
        Bwb = [bufs(KD, "adawb%d_" % i) for i in range(2)]
        scb = ar.alloc("silu_cb", [128, KD, 4], BF16)
        c.dma(c3[:], c3T.rearrange("(k p) t -> p k t", p=128), writes=[Bc3])
        c.I("act", "activation", dict(out=sc[:], in_=c3[:], func=AF.Silu), reads=[Bc3], writes=[Bsc])
        c.I("dve", "tensor_copy", dict(out=scb[:], in_=sc[:]), reads=[Bsc], writes=[Bsc])
        it = 0
        for l in layers:
            aw = W["l%d_ada_w" % l]
            for j in range(6):
                s = it % 2
                it += 1
                c.dma(wst[s][:], aw[:, j * 1024:(j + 1) * 1024].rearrange("(k p) n -> p k n", p=128), writes=[Bw[s]])
                for kk in range(KD):
                    cast_op(wsb[s][:, kk, :], wst[s][:, kk, :], [Bw[s]], [Bwb[s][kk]])
                for k in range(KD):
                    bank = (it * KD + k) % 8
                    mm_group(psum[:, bank, 0:4], PB[bank],
                             [(wsb[s][:, kk, k * 128:(k + 1) * 128], scb[:, kk, :], [Bwb[s][kk], Bsc]) for kk in range(KD)])
                    i0 = ((l * 6 + j) * KD + k) * 4
                    o, _ = cvoff["l%d_ada_b" % l]
                    c.I("dve", "tensor_scalar", dict(
                        out=mod[:, i0:i0 + 4], in0=psum[:, bank, 0:4], scalar1=cv[:, o + j * 8 + k:o + j * 8 + k + 1], scalar2=None, op0=ALU.add),
                        reads=[PB[bank], Bcv], writes=[Bmod])
            for which, (gname, jsc) in enumerate((("norm1", 1), ("norm2", 4))):
                go, _ = cvoff["l%d_%s" % (l, gname)]
                for k in range(KD):
                    i0 = ((l * 6 + jsc) * KD + k) * 4
                    a0 = ((l * 2 + which) * KD + k) * 4
                    c.I("dve", "tensor_scalar", dict(
                        out=modA[:, a0:a0 + 4], in0=mod[:, i0:i0 + 4], scalar1=1.0, scalar2=cv[:, go + k:go + k + 1],
                        op0=ALU.add, op1=ALU.mult), reads=[Bmod, Bcv], writes=[Bmod])
                    c.I("dve", "tensor_scalar", dict(
                        out=modA[:, a0:a0 + 4], in0=modA[:, a0:a0 + 4], scalar1=32.0, scalar2=None, op0=ALU.mult),
                        reads=[Bmod], writes=[Bmod])
        if debug:
            c.dma(modo, mod[:], reads=[Bmod])
        c.barrier()
        ar.reset(m0)

    def make_tiles(with_ctx):
        tiles = []
        for b in range(NB):
            for i in range(SEQ // T):
                tiles.append(dict(kind="lat", tok=b, g0=b * SEQ + i * T, first=(i == 0), last=(i == SEQ // T - 1)))
        if with_ctx:
            for b in range(NB):
                tiles.append(dict(kind="ctx", tok=2, g0=b * NCTX, first=True, last=True))
        return tiles

    WH = T + 2

    class NormPipe:
        def __init__(self, l, which, src_lat, src_ctx, nh, nx, Wt=WH, halo=True, tt_eng="dve"):
            self.l, self.which = l, which
            self.tt_eng = tt_eng
            self.W, self.halo = Wt, halo
            self.src = {"lat": src_lat, "ctx": src_ctx}
            self.h = [ar.alloc("h%d" % i, [128, KD, Wt], F32) for i in range(nh)]
            self.Bh = bufs(nh, "h")
            self.xn = [ar.alloc("xn%d" % i, [128, KD, Wt], BF16) for i in range(nx)]
            self.Bxn = [bufs(KD, "xn%d_" % i) for i in range(nx)]
            self.sq = [ar.alloc("sq%d" % i, [128, Wt], BF16) for i in range(4)]
            self.Bsq = bufs(4, "sq")
            self.rs = [ar.alloc("rs%d" % i, [128, Wt], F32) for i in range(2)]
            self.Brs = bufs(2, "rs")
            self.tt = [ar.alloc("tt%d" % i, [128, Wt], F32) for i in range(2)]
            self.Btt = bufs(2, "tt")
            self.nsq = 0
            self.ntt = 0

        def load(self, i, t):
            hb, B = self.h[i % len(self.h)], self.Bh[i % len(self.h)]
            src = self.src[t["kind"]]
            Wt = self.W
            if not self.halo:
                c.dma(hb[:], src[:, t["g0"]:t["g0"] + Wt].rearrange("(k p) n -> p k n", p=128), writes=[B])
                return
            lo = 1 if t["first"] else 0
            hi = Wt - 1 if t["last"] else Wt
            g0 = t["g0"] - 1
            c.dma(hb[:, :, lo:hi], src[:, g0 + lo:g0 + hi].rearrange("(k p) n -> p k n", p=128), writes=[B])
            if t["first"]:
                c.I("pool", "memset", dict(ap=hb[:, :, 0:1], constant=0.0), writes=[B])
            if t["last"]:
                c.I("pool", "memset", dict(ap=hb[:, :, Wt - 1:Wt], constant=0.0), writes=[B])

        def sq_k(self, i, k):
            hb, B = self.h[i % len(self.h)], self.Bh[i % len(self.h)]
            s = k % 4
            c.I("act", "activation", dict(out=self.sq[s][:], in_=hb[:, k, :], func=AF.Square),
                reads=[B], writes=[self.Bsq[s]])

        def ss_k(self, i, k, ssbank):
            s = k % 4
            c.I("pe", "matmul", dict(out=psum[:, ssbank, 0:self.W], lhsT=ones_bf[:], rhs=self.sq[s][:],
                                     start=(k == 0), stop=(k == KD - 1)),
                reads=[self.Bsq[s], Bones], writes=[PB[ssbank]], inc=True)

        def rs_fin(self, i, ssbank):
            r = i % 2
            c.I("act", "activation", dict(out=self.rs[r][:], in_=psum[:, ssbank, 0:self.W], func=AF.Ln, bias=EPS * D, scale=1.0),
                reads=[PB[ssbank]], writes=[self.Brs[r]])
            c.I("act", "activation", dict(out=self.rs[r][:], in_=self.rs[r][:], func=AF.Exp, scale=-0.5),
                reads=[self.Brs[r]], writes=[self.Brs[r]])

        def part1(self, i, t, ssbank):
            for k in range(KD):
                self.sq_k(i, k)
                self.ss_k(i, k, ssbank)
            self.rs_fin(i, ssbank)

        def part2_k(self, i, t, k):
            hb, B = self.h[i % len(self.h)], self.Bh[i % len(self.h)]
            x, Bx = self.xn[i % len(self.xn)], self.Bxn[i % len(self.xn)]
            r = i % 2
            jsh = 0 if self.which == 0 else 3
            s = k % 2
            c.I(self.tt_eng, "tensor_tensor", dict(out=self.tt[s][:], in0=hb[:, k, :], in1=self.rs[r][:], op=ALU.mult),
                reads=[B, self.Brs[r]], writes=[self.Btt[s]])
            c.I("act", "activation", dict(out=x[:, k, :], in_=self.tt[s][:], func=AF.Identity,
                                          scale=modA_ap(self.l, self.which, k, t["tok"]),
                                          bias=mod_ap(self.l, jsh, k, t["tok"])),
                reads=[self.Btt[s], Bmod], writes=[Bx[k]])

        def part2(self, i, t):
            for k in range(KD):
                self.part2_k(i, t, k)

        def spread(self, step, i, t, ssbank, per=1, start=0):
            step -= start
            if step < 0:
                return
            n1 = KD // per
            if 1 <= step < n1 + 1:
                for k in range((step - 1) * per, step * per):
                    self.ss_k(i, k, ssbank)
            if step < n1:
                for k in range(step * per, (step + 1) * per):
                    self.sq_k(i, k)
            if step == n1 + 1:
                self.rs_fin(i, ssbank)
            if n1 + 2 <= step < 2 * n1 + 2:
                for k in range((step - n1 - 2) * per, (step - n1 - 1) * per):
                    self.part2_k(i, t, k)

        def hbuf(self, i):
            return self.h[i % len(self.h)], self.Bh[i % len(self.h)]

        def xnbuf(self, i):
            return self.xn[i % len(self.xn)], self.Bxn[i % len(self.xn)]

    def stage_sconv(l, src_lat, src_ctx, dst_lat, dst_ctx, with_ctx):
        c.stage = "L%d_sconv" % l
        m0 = ar.mark()
        p = "l%d_" % l
        win = ar.alloc("win", [128, KD, 3 * D], BF16)
        wout = ar.alloc("wout", [128, KD, D], BF16)
        Bwin, Bwout = Buf("win"), Buf("wout")
        m1 = ar.mark()
        stg = [ar.alloc("stg%d" % i, [128, 2048], F32) for i in range(4)]
        Bstg = bufs(4, "stg")
        load_weight(win, Bwin, W[p + "sc_in"], KD, 3 * D, stg, Bstg)
        load_weight(wout, Bwout, W[p + "sc_out"], KD, D, stg, Bstg)
        c.barrier()
        ar.reset(m1)
        npipe = NormPipe(l, 0, src_lat, src_ctx, nh=3, nx=2)
        PH7 = bufs(2, "ps7h")
        cgs = [ar.alloc("cgs%d" % i, [128, WH], F32) for i in range(2)]
        Bcgs = bufs(2, "cgs")
        ch = [ar.alloc("ch%d" % i, [128, WH], F32) for i in range(2)]
        Bch = bufs(2, "ch")
        acc = [ar.alloc("acc%d" % i, [128, T], F32) for i in range(2)]
        Bacc = bufs(2, "acc")
        v = [ar.alloc("v%d" % i, [128, KD, T], BF16) for i in range(2)]
        Bv = [bufs(KD, "v%d_" % i) for i in range(2)]
        yo = [ar.alloc("yo%d" % i, [128, KD, T], F32) for i in range(2)]
        Byo = bufs(2, "yo")
        tiles = make_tiles(with_ctx)
        dst = {"lat": dst_lat, "ctx": dst_ctx}
        cwo, _ = cvoff[p + "sc_conv"]
        SS = 0

        def cw(tap, m):
            return cv[:, cwo + tap * KD + m:cwo + tap * KD + m + 1]

        n = len(tiles)
        npipe.load(0, tiles[0])
        if n > 1:
            npipe.load(1, tiles[1])
        npipe.part1(0, tiles[0], SS)
        npipe.part2(0, tiles[0])
        it = 0
        for i, t in enumerate(tiles):
            if i + 2 < n:
                npipe.load(i + 2, tiles[i + 2])
            x, Bx = npipe.xnbuf(i)
            hb, Bh = npipe.hbuf(i)
            vv, Bvv = v[i % 2], Bv[i % 2]
            def sc_p1(m):
                nonlocal it
                st = it % 2
                it += 1
                bcg, bhh, bb = 1 + 3 * st, 2 + 3 * st, 3 + 3 * st
                s = m % 2
                mm_group(psum[:, bcg, 0:WH], PB[bcg],
                         [(win[:, k, D + m * 128:D + (m + 1) * 128], x[:, k, :], [Bwin, Bx[k]]) for k in range(KD)])
                mm_group(psum[:, bhh, 0:WH], PB[bhh],
                         [(win[:, k, 2 * D + m * 128:2 * D + (m + 1) * 128], x[:, k, :], [Bwin, Bx[k]]) for k in range(KD)])
                c.I("act", "activation", dict(out=cgs[s][:], in_=psum[:, bcg, 0:WH], func=AF.Copy),
                    reads=[PB[bcg]], writes=[Bcgs[s]])
                c.I("dve", "tensor_tensor", dict(out=ch[s][:], in0=cgs[s][:], in1=psum[:, bhh, 0:WH], op=ALU.mult),
                    reads=[Bcgs[s], PB[bhh]], writes=[Bch[s]])
                if t["first"]:
                    c.I("pool", "memset", dict(ap=ch[s][:, 0:1], constant=0.0), writes=[Bch[s]])
                if t["last"]:
                    c.I("pool", "memset", dict(ap=ch[s][:, WH - 1:WH], constant=0.0), writes=[Bch[s]])
                mm_group(psum[:, bb, 0:T], PB[bb],
                         [(win[:, k, m * 128:(m + 1) * 128], x[:, k, 1:T + 1], [Bwin, Bx[k]]) for k in range(KD)])
                return bb

            def sc_p2(m, bb):
                s = m % 2
                c.I("act", "activation", dict(out=acc[s][:], in_=ch[s][:, 0:T], func=AF.Copy, scale=cw(0, m)),
                    reads=[Bch[s], Bcv], writes=[Bacc[s]])
                c.I("dve", "scalar_tensor_tensor", dict(out=acc[s][:], in0=ch[s][:, 1:T + 1], scalar=cw(1, m), in1=acc[s][:],
                                                        op0=ALU.mult, op1=ALU.add),
                    reads=[Bch[s], Bacc[s], Bcv], writes=[Bacc[s]])
                c.I("dve", "scalar_tensor_tensor", dict(out=acc[s][:], in0=ch[s][:, 2:T + 2], scalar=cw(2, m), in1=acc[s][:],
                                                        op0=ALU.mult, op1=ALU.add),
                    reads=[Bch[s], Bacc[s], Bcv], writes=[Bacc[s]])
                c.I("dve", "tensor_tensor", dict(out=vv[:, m, :], in0=acc[s][:], in1=psum[:, bb, 0:T], op=ALU.mult),
                    reads=[Bacc[s], PB[bb]], writes=[Bvv[m]])

            prev = None
            for m in range(KD + 1):
                if i + 1 < n:
                    npipe.spread(m, i + 1, tiles[i + 1], SS, per=4, start=0)
                cur = None
                if m < KD:
                    cur = (m, sc_p1(m))
                if prev is not None:
                    sc_p2(*prev)
                prev = cur
            yy, Byy = yo[i % 2], Byo[i % 2]
            for nn in range(KD):
                bo = (7, 1, 2, 3, 4, 5, 6, 7)[nn]
                mm_group(psum[:, bo, 0:T], PB[bo],
                         [(wout[:, m, nn * 128:(nn + 1) * 128], vv[:, m, :], [Bwout, Bvv[m]]) for m in range(KD)])
                c.I("dve", "scalar_tensor_tensor", dict(out=yy[:, nn, :], in0=psum[:, bo, 0:T], scalar=mod_ap(l, 2, nn, t["tok"]),
                                                                      in1=hb[:, nn, 1:T + 1], op0=ALU.mult, op1=ALU.add),
                     reads=[PB[bo], Bh, Bmod], writes=[Byy])
            c.dma(dst[t["kind"]][:, t["g0"]:t["g0"] + T].rearrange("(k p) n -> p k n", p=128), yy[:], reads=[Byy])
        c.barrier()
        ar.reset(m0)

    def stage_ffn(l, src_lat, src_ctx, dst_lat, dst_ctx, with_ctx):
        c.stage = "L%d_ffn" % l
        m0 = ar.mark()
        p = "l%d_" % l
        wup = ar.alloc("wup", [128, KD, 2 * DFF], BF16)
        wdn = ar.alloc("wdn", [128, KF, D], BF16)
        Bwup, Bwdn = Buf("wup"), Buf("wdn")
        m1 = ar.mark()
        stg = [ar.alloc("stg%d" % i, [128, 2048], F32) for i in range(4)]
        Bstg = bufs(4, "stg")
        load_weight(wup, Bwup, W[p + "ffn_up"], KD, 2 * DFF, stg, Bstg)
        load_weight(wdn, Bwdn, W[p + "ffn_down"], KF, D, stg, Bstg)
        c.barrier()
        ar.reset(m1)
        npipe = NormPipe(l, 1, src_lat, src_ctx, nh=2, nx=2, tt_eng="pool")
        t0 = [ar.alloc("t0_%d" % i, [128, T], F32) for i in range(2)]
        Bt0 = bufs(2, "t0")
        sg = [ar.alloc("sg%d" % i, [128, T], F32) for i in range(2)]
        Bsg = bufs(2, "sg")
        u = [ar.alloc("u%d" % i, [128, KF, T], BF16) for i in range(2)]
        Bu = [bufs(KF, "u%d_" % i) for i in range(2)]
        yo = [ar.alloc("yo%d" % i, [128, KD, T], F32) for i in range(1)]
        Byo = bufs(1, "yo")
        tiles = make_tiles(with_ctx)
        dst = {"lat": dst_lat, "ctx": dst_ctx}
        cwo, _ = cvoff[p + "ffn_conv_w"]
        cbo, _ = cvoff[p + "ffn_conv_b"]
        SS = 6

        def cw(tap, j):
            return cv[:, cwo + tap * KF + j:cwo + tap * KF + j + 1]

        n = len(tiles)
        npipe.load(0, tiles[0])
        if n > 1:
            npipe.load(1, tiles[1])
        npipe.part1(0, tiles[0], SS)
        npipe.part2(0, tiles[0])
        it = 0
        for i, t in enumerate(tiles):
            x, Bx = npipe.xnbuf(i)
            hb, Bh = npipe.hbuf(i)
            uu, Buu = u[i % 2], Bu[i % 2]
            def ffn_p1(j):
                nonlocal it
                st = it % 3
                it += 1
                bg = 1 + st
                ba = (0, 4, 5)[it % 3]
                s = j % 2
                mm_group(psum[:, bg, 0:WH], PB[bg],
                         [(wup[:, k, DFF + j * 128:DFF + (j + 1) * 128], x[:, k, :], [Bwup, Bx[k]]) for k in range(KD)])
                if t["first"]:
                    c.I("dve", "memset", dict(ap=psum[:, bg, 0:1], constant=0.0), writes=[PB[bg]])
                if t["last"]:
                    c.I("dve", "memset", dict(ap=psum[:, bg, WH - 1:WH], constant=0.0), writes=[PB[bg]])
                c.I("act", "activation", dict(out=t0[s][:], in_=psum[:, bg, 0:T], func=AF.Copy, scale=cw(0, j)),
                    reads=[PB[bg], Bcv], writes=[Bt0[s]])
                c.I("dve", "scalar_tensor_tensor", dict(out=t0[s][:], in0=psum[:, bg, 1:T + 1], scalar=cw(1, j), in1=t0[s][:],
                                                        op0=ALU.mult, op1=ALU.add),
                    reads=[PB[bg], Bt0[s], Bcv], writes=[Bt0[s]])
                c.I("dve", "scalar_tensor_tensor", dict(out=t0[s][:], in0=psum[:, bg, 2:T + 2], scalar=cw(2, j), in1=t0[s][:],
                                                        op0=ALU.mult, op1=ALU.add),
                    reads=[PB[bg], Bt0[s], Bcv], writes=[Bt0[s]])
                mm_group(psum[:, ba, 0:T], PB[ba],
                         [(wup[:, k, j * 128:(j + 1) * 128], x[:, k, 1:T + 1], [Bwup, Bx[k]]) for k in range(KD)])
                return ba

            def ffn_p2(j, ba):
                s = j % 2
                s2 = j % 2
                c.I("act", "activation", dict(out=sg[s2][:], in_=t0[s][:], func=AF.Silu, bias=cv[:, cbo + j:cbo + j + 1]),
                    reads=[Bt0[s], Bcv], writes=[Bsg[s2]])
                c.I("dve", "tensor_tensor", dict(out=uu[:, j, :], in0=sg[s2][:], in1=psum[:, ba, 0:T], op=ALU.mult),
                    reads=[Bsg[s2], PB[ba]], writes=[Buu[j]])

            prev = None
            for j in range(KF + 1):
                if i + 1 < n:
                    npipe.spread(j, i + 1, tiles[i + 1], SS, per=2, start=9)
                cur = None
                if j < KF:
                    cur = (j, ffn_p1(j))
                if prev is not None:
                    ffn_p2(*prev)
                prev = cur
            yy, Byy = yo[0], Byo[0]
            for nn in range(KD):
                bo = 6 + (nn % 2)
                mm_group(psum[:, bo, 0:T], PB[bo],
                         [(wdn[:, j, nn * 128:(nn + 1) * 128], uu[:, j, :], [Bwdn, Buu[j]]) for j in range(KF)])
                c.I("dve", "scalar_tensor_tensor", dict(out=yy[:, nn, :], in0=psum[:, bo, 0:T], scalar=mod_ap(l, 5, nn, t["tok"]),
                                                                      in1=hb[:, nn, 1:T + 1], op0=ALU.mult, op1=ALU.add),
                     reads=[PB[bo], Bh, Bmod], writes=[Byy])
            c.dma(dst[t["kind"]][:, t["g0"]:t["g0"] + T].rearrange("(k p) n -> p k n", p=128), yy[:], reads=[Byy])
            if i + 2 < n:
                npipe.load(i + 2, tiles[i + 2])
        c.barrier()
        ar.reset(m0)


    TQ = 512

    def stage_qkv(l, kind, src_lat, src_ctx, need_qctx):
        c.stage = "L%d_qkv" % l
        m0 = ar.mark()
        p = "l%d_" % l
        if kind == 1:
            wname, NQC, NKC, VC = "wa_qkv", 8, 2, 256
        else:
            wname, NQC, NKC, VC = "da_qkv", 8, 8, 1024
        NCOL = (NQC + NKC) * 128 + VC
        VOFF = (NQC + NKC) * 128
        wq = ar.alloc("wqkv", [128, KD, NCOL], BF16)
        Bwq = Buf("wqkv")
        m1 = ar.mark()
        stg = [ar.alloc("stg%d" % i, [128, 2048], F32) for i in range(4)]
        Bstg = bufs(4, "stg")
        load_weight(wq, Bwq, W[p + wname], KD, NCOL, stg, Bstg)
        c.barrier()
        ar.reset(m1)
        cosb = ar.alloc("cosb", [128, SEQ], F32)
        sinb = ar.alloc("sinb", [128, SEQ], F32)
        Btab = Buf("tab")
        c.dma(cosb[:], cos_d, writes=[Btab])
        c.dma(sinb[:], sin_d, writes=[Btab])
        rp32 = ar.alloc("rp32", [128, 128], F32)
        rpb = ar.alloc("rpb", [128, 128], BF16)
        ones2 = ar.alloc("ones2", [128, 128], BF16)
        g8 = ar.alloc("g8", [128, 2], F32)
        Bk = Buf("kconst")
        c.dma(rp32[:], rperm_d, writes=[Bk])
        c.I("dve", "tensor_copy", dict(out=rpb[:], in_=rp32[:]), reads=[Bk], writes=[Bk])
        c.I("pool", "memset", dict(ap=ones2[:], constant=0.0), writes=[Bk])
        c.I("pool", "memset", dict(ap=ones2[0:64, 0:64], constant=1.0), writes=[Bk])
        c.I("pool", "memset", dict(ap=ones2[64:128, 64:128], constant=1.0), writes=[Bk])
        c.I("dve", "tensor_scalar", dict(out=g8[:, 0:1], in0=cvc(p + "qg"), scalar1=8.0, scalar2=None, op0=ALU.mult), reads=[Bcv], writes=[Bk])
        c.I("dve", "tensor_scalar", dict(out=g8[:, 1:2], in0=cvc(p + "kg"), scalar1=8.0, scalar2=None, op0=ALU.mult), reads=[Bcv], writes=[Bk])
        npipe = NormPipe(l, 0, src_lat, src_ctx, nh=2, nx=2, Wt=TQ, halo=False)
        sqc = [ar.alloc("sqc%d" % i, [128, TQ], BF16) for i in range(3)]
        Bsqc = bufs(3, "sqc")
        rsc = [ar.alloc("rsc%d" % i, [128, TQ], F32) for i in range(2)]
        Brsc = bufs(2, "rsc")
        qn = [ar.alloc("qn%d" % i, [128, TQ], BF16) for i in range(4)]
        Bqn = bufs(4, "qn")
        t1 = [ar.alloc("t1_%d" % i, [128, TQ], F32) for i in range(2)]
        Bt1 = bufs(2, "t1")
        t2 = [ar.alloc("t2_%d" % i, [128, TQ], F32) for i in range(2)]
        Bt2 = bufs(2, "t2")
        qr = [ar.alloc("qr%d" % i, [128, TQ], BF16) for i in range(3)]
        Bqr = bufs(3, "qr")
        vt = [ar.alloc("vt%d" % i, [128, VC], BF16) for i in range(2)]
        Bvt = bufs(2, "vt")
        tiles = []
        for b in range(NB):
            for i in range(SEQ // TQ):
                tiles.append(dict(kind="lat", tok=b, g0=b * SEQ + i * TQ, pos=i * TQ))
        tiles.append(dict(kind="ctx", tok=2, g0=0, pos=0))
        n = len(tiles)
        SS = 0
        npipe.load(0, tiles[0])
        npipe.load(1, tiles[1])
        npipe.part1(0, tiles[0], SS)
        npipe.part2(0, tiles[0])
        items = []
        for i, t in enumerate(tiles):
            lat = t["kind"] == "lat"
            chunks = [("k", ci) for ci in range(NKC)]
            if lat or need_qctx:
                chunks += [("q", ci) for ci in range(NQC)]
            for cidx, (qk, ci) in enumerate(chunks):
                items.append(dict(i=i, t=t, lat=lat, cidx=cidx, qk=qk, ci=ci, lastc=(cidx == len(chunks) - 1)))
        nv = [0]

        def phA(m):
            it_ = items[m]
            x, Bx = npipe.xnbuf(it_["i"])
            bq = 1 + m % 3
            col0 = (it_["ci"] if it_["qk"] == "q" else NQC + it_["ci"]) * 128
            mm_group(psum[:, bq, 0:TQ], PB[bq],
                     [(wq[:, k, col0:col0 + 128], x[:, k, :], [Bwq, Bx[k]]) for k in range(KD)])
            c.I("act", "activation", dict(out=sqc[m % 3][:], in_=psum[:, bq, 0:TQ], func=AF.Square),
                reads=[PB[bq]], writes=[Bsqc[m % 3]])

        def phB(m):
            it_ = items[m]
            bq = 1 + m % 3
            bs = 4 + m % 2
            mm_group(psum[:, bs, 0:TQ], PB[bs], [(ones2[:], sqc[m % 3][:], [Bk, Bsqc[m % 3]])])
            c.I("act", "activation", dict(out=rsc[m % 2][:], in_=psum[:, bs, 0:TQ], func=AF.Ln, bias=EPS * HD, scale=1.0),
                reads=[PB[bs]], writes=[Brsc[m % 2]])
            c.I("act", "activation", dict(out=rsc[m % 2][:], in_=rsc[m % 2][:], func=AF.Exp, scale=-0.5),
                reads=[Brsc[m % 2]], writes=[Brsc[m % 2]])
            gcol = 0 if it_["qk"] == "q" else 1
            c.I("dve", "scalar_tensor_tensor", dict(out=qn[m % 4][:], in0=psum[:, bq, 0:TQ], scalar=g8[:, gcol:gcol + 1], in1=rsc[m % 2][:],
                                                    op0=ALU.mult, op1=ALU.mult),
                reads=[PB[bq], Brsc[m % 2], Bk], writes=[Bqn[m % 4]])

        def phC(m):
            it_ = items[m]
            t = it_["t"]
            lat = it_["lat"]
            qk, ci = it_["qk"], it_["ci"]
            dst = {"q": (QT_lat if lat else QT_ctx), "k": (KT_lat if lat else KT_ctx)}[qk]
            drow = ci * 128
            q4 = m % 4
            if lat:
                bp = 6
                mm_group(psum[:, bp, 0:TQ], PB[bp], [(rpb[:], qn[q4][:], [Bk, Bqn[q4]])])
                pos = t["pos"]
                c.I("dve", "tensor_tensor", dict(out=t1[m % 2][:], in0=qn[q4][:], in1=cosb[:, pos:pos + TQ], op=ALU.mult),
                    reads=[Bqn[q4], Btab], writes=[Bt1[m % 2]])
                c.I("dve", "tensor_tensor", dict(out=t2[m % 2][:], in0=psum[:, bp, 0:TQ], in1=sinb[:, pos:pos + TQ], op=ALU.mult),
                    reads=[PB[bp], Btab], writes=[Bt2[m % 2]])
                c.I("pool", "tensor_tensor", dict(out=qr[m % 3][:], in0=t1[m % 2][:], in1=t2[m % 2][:], op=ALU.add),
                    reads=[Bt1[m % 2], Bt2[m % 2]], writes=[Bqr[m % 3]])
                c.dma(dst[drow:drow + 128, t["g0"]:t["g0"] + TQ], qr[m % 3][:], reads=[Bqr[m % 3]])
            else:
                c.dma(dst[drow:drow + 128, t["g0"]:t["g0"] + TQ], qn[q4][:], reads=[Bqn[q4]])

        def emitV(i):
            t = tiles[i]
            x, Bx = npipe.xnbuf(i)
            vdst = V_lat if t["kind"] == "lat" else V_ctx
            for sub in range(TQ // 128):
                sv = nv[0] % 2
                nv[0] += 1
                for c0 in range(0, VC, 512):
                    cw_ = min(512, VC - c0)
                    bv = 7
                    mm_group(psum[:, bv, 0:cw_], PB[bv],
                             [(x[:, k, sub * 128:(sub + 1) * 128], wq[:, k, VOFF + c0:VOFF + c0 + cw_], [Bwq, Bx[k]]) for k in range(KD)])
                    c.I("act", "activation", dict(out=vt[sv][:, c0:c0 + cw_], in_=psum[:, bv, 0:cw_], func=AF.Copy),
                        reads=[PB[bv]], writes=[Bvt[sv]])
                r0 = t["g0"] + sub * 128
                c.dma(vdst[r0:r0 + 128, 0:VC], vt[sv][:], reads=[Bvt[sv]])

        M = len(items)
        for m in range(M + 2):
            if m < M:
                it_ = items[m]
                i = it_["i"]
                if it_["cidx"] == 2 and i + 1 < n:
                    npipe.part1(i + 1, tiles[i + 1], SS)
                if it_["cidx"] == 5 and i + 1 < n:
                    npipe.part2(i + 1, tiles[i + 1])
                phA(m)
            if 0 <= m - 1 < M:
                phB(m - 1)
            if 0 <= m - 2 < M:
                phC(m - 2)
            if m < M and items[m]["lastc"]:
                i = items[m]["i"]
                emitV(i)
                if i + 2 < n:
                    npipe.load(i + 2, tiles[i + 2])
        c.barrier()
        ar.reset(m0)

    def stage_diffattn(l):
        c.stage = "L%d_dattn" % l
        m0 = ar.mark()
        p = "l%d_" % l
        lam_init = 0.8 - 0.6 * math.exp(-0.3 * l)
        NKCH = (SEQ + NCTX) // 128
        NQT = SEQ // 512
        Bsc = Buf("dsc")
        prod = ar.alloc("prod", [128, 128], F32)
        sc_ = ar.alloc("dsc", [128, 8], F32)
        o1, _ = cvoff[p + "lq1"]
        o2, _ = cvoff[p + "lk1"]
        o3, _ = cvoff[p + "lq2"]
        o4, _ = cvoff[p + "lk2"]
        c.I("dve", "tensor_tensor", dict(out=prod[:, 0:64], in0=cv[:, o1:o1 + 64], in1=cv[:, o2:o2 + 64], op=ALU.mult), reads=[Bcv], writes=[Bsc])
        c.I("dve", "tensor_tensor", dict(out=prod[:, 64:128], in0=cv[:, o3:o3 + 64], in1=cv[:, o4:o4 + 64], op=ALU.mult), reads=[Bcv], writes=[Bsc])
        c.I("dve", "reduce_sum", dict(out=sc_[:, 0:1], in_=prod[:, 0:64], axis=mybir.AxisListType.X), reads=[Bsc], writes=[Bsc])
        c.I("dve", "reduce_sum", dict(out=sc_[:, 1:2], in_=prod[:, 64:128], axis=mybir.AxisListType.X), reads=[Bsc], writes=[Bsc])
        c.I("act", "activation", dict(out=sc_[:, 2:4], in_=sc_[:, 0:2], func=AF.Exp), reads=[Bsc], writes=[Bsc])
        c.I("dve", "scalar_tensor_tensor", dict(out=sc_[:, 4:5], in0=sc_[:, 3:4], scalar=-lam_init, in1=sc_[:, 2:3], op0=ALU.add, op1=ALU.subtract),
            reads=[Bsc], writes=[Bsc])
        c.I("dve", "tensor_scalar", dict(out=sc_[:, 5:6], in0=cvc(p + "subln"), scalar1=(1.0 - lam_init), scalar2=None, op0=ALU.mult),
            reads=[Bcv], writes=[Bsc])
        ones32 = ar.alloc("ones32", [128, 128], F32)
        c.I("pool", "memset", dict(ap=ones32[:], constant=1.0), writes=[Bsc])
        kth = [ar.alloc("kth%d" % i, [128, SEQ + NCTX], BF16) for i in range(2)]
        vh = [ar.alloc("vh%d" % i, [128, NKCH, 128], BF16) for i in range(2)]
        qth = [ar.alloc("qth%d" % i, [128, SEQ], BF16) for i in range(2)]
        Bkv = bufs(2, "kvq")
        E = [ar.alloc("E%d" % i, [128, 2, 512], BF16) for i in range(3)]
        BE = bufs(3, "E")
        osb = [ar.alloc("osb%d" % i, [128, 512], F32) for i in range(2)]
        lsb = [ar.alloc("lsb%d" % i, [128, 512], F32) for i in range(2)]
        Bol = Buf("ol")
        od = ar.alloc("od", [128, 512], F32)
        sq32 = ar.alloc("sq32", [128, 512], F32)
        rsd = ar.alloc("rsd", [128, 512], F32)
        Bep = Buf("ep")
        ob = [ar.alloc("ob%d" % i, [128, 512], BF16) for i in range(2)]
        Bob = bufs(2, "ob")

        def load_bh(idx, b, h):
            s_ = idx % 2
            r0 = h * 128
            c.dma(kth[s_][:, 0:SEQ], KT_lat[r0:r0 + 128, b * SEQ:(b + 1) * SEQ], writes=[Bkv[s_]])
            c.dma(kth[s_][:, SEQ:SEQ + NCTX], KT_ctx[r0:r0 + 128, b * NCTX:(b + 1) * NCTX], writes=[Bkv[s_]])
            c.dma(vh[s_][:, 0:SEQ // 128, :], V_lat[b * SEQ:(b + 1) * SEQ, r0:r0 + 128].rearrange("(j p) e -> p j e", p=128), writes=[Bkv[s_]])
            c.dma(vh[s_][:, SEQ // 128:NKCH, :], V_ctx[b * NCTX:(b + 1) * NCTX, r0:r0 + 128].rearrange("(j p) e -> p j e", p=128), writes=[Bkv[s_]])
            c.dma(qth[s_][:], QT_lat[r0:r0 + 128, b * SEQ:(b + 1) * SEQ], writes=[Bkv[s_]])

        bhs = [(b, h) for b in range(NB) for h in range(8)]
        load_bh(0, *bhs[0])
        steps = []
        for idx, (b, h) in enumerate(bhs):
            for qt in range(NQT):
                for j in range(NKCH):
                    steps.append((idx, b, h, qt, j))
        nq = [0]

        def emit_S(n):
            idx, b, h, qt, j = steps[n]
            s_ = idx % 2
            q0 = qt * 512
            A = (n % 2) * 2
            es = n % 3
            for cc in range(2):
                c.I("pe", "matmul", dict(out=psum[:, A + cc, :], lhsT=kth[s_][cc * 64:(cc + 1) * 64, j * 128:(j + 1) * 128],
                                         rhs=qth[s_][cc * 64:(cc + 1) * 64, q0:q0 + 512], start=True, stop=True),
                    reads=[Bkv[s_]], writes=[PB[A + cc]], inc=(cc == 1))
            c.I("act", "activation", dict(out=E[es][:], in_=psum[:, A:A + 2, :], func=AF.Exp, scale=0.125),
                reads=[PB[A], PB[A + 1]], writes=[BE[es]])

        sel = ar.alloc("sel", [64, 2, 128], F32)
        c.I("pool", "memset", dict(ap=sel[:], constant=0.0), writes=[Bsc])
        c.I("pool", "memset", dict(ap=sel[0:1, 0, :], constant=1.0), writes=[Bsc])
        c.I("pool", "memset", dict(ap=sel[32:33, 1, :], constant=1.0), writes=[Bsc])
        l64 = ar.alloc("l64", [64, 512], F32)
        Bl64 = Buf("l64")
        DL = (6, 8, 14) if NKCH >= 20 else (2, 3, 5)
        defer = []

        def emit_OL(n):
            idx, b, h, qt, j = steps[n]
            s_ = idx % 2
            es = n % 3
            last = (j == NKCH - 1)
            for cc in range(2):
                c.I("pe", "matmul", dict(out=psum[:, 4 + cc, :], lhsT=vh[s_][:, j, :], rhs=E[es][:, cc, :], start=(j == 0), stop=last),
                    reads=[Bkv[s_], BE[es]], writes=[PB[4 + cc]], inc=last)
            for cc in range(2):
                c.I("pe", "matmul", dict(out=psum[32 * cc:32 * cc + 32, 6, :], lhsT=ones_bf[:, 0:32], rhs=E[es][:, cc, :], start=(j == 0), stop=last),
                    reads=[Bones, BE[es]], writes=[PB[6]], inc=(last and cc == 1))
            if not last:
                return
            q0 = qt * 512
            for cc in range(2):
                c.I("dve", "tensor_copy", dict(out=osb[cc][:], in_=psum[:, 4 + cc, :]), reads=[PB[4 + cc]], writes=[Bol])
            c.I("dve", "tensor_copy", dict(out=l64[:], in_=psum[0:64, 6, :]), reads=[PB[6]], writes=[Bl64])
            c.I("dve", "reciprocal", dict(out=l64[:], in_=l64[:]), reads=[Bl64], writes=[Bl64])

            def ep_a():
                c.I("pe", "matmul", dict(out=psum[:, 7, :], lhsT=sel[:, 0, :], rhs=l64[:], start=True, stop=True),
                    reads=[Bsc, Bl64], writes=[PB[7]], inc=True)
                c.I("dve", "tensor_tensor", dict(out=osb[0][:], in0=osb[0][:], in1=psum[:, 7, :], op=ALU.mult), reads=[Bol, PB[7]], writes=[Bol])

            def ep_b():
                c.I("pe", "matmul", dict(out=psum[:, 7, :], lhsT=sel[:, 1, :], rhs=l64[:], start=True, stop=True),
                    reads=[Bsc, Bl64], writes=[PB[7]], inc=True)
                c.I("dve", "tensor_tensor", dict(out=osb[1][:], in0=osb[1][:], in1=psum[:, 7, :], op=ALU.mult), reads=[Bol, PB[7]], writes=[Bol])
                c.I("dve", "scalar_tensor_tensor", dict(out=od[:], in0=osb[1][:], scalar=sc_[:, 4:5], in1=osb[0][:], op0=ALU.mult, op1=ALU.add),
                    reads=[Bol, Bsc], writes=[Bep])
                c.I("pool", "tensor_tensor", dict(out=sq32[:], in0=od[:], in1=od[:], op=ALU.mult), reads=[Bep], writes=[Bep])

            def ep_c():
                c.I("pe", "matmul", dict(out=psum[:, 7, :], lhsT=ones32[:], rhs=sq32[:], start=True, stop=True),
                    reads=[Bsc, Bep], writes=[PB[7]], inc=True)
                c.I("act", "activation", dict(out=rsd[:], in_=psum[:, 7, :], func=AF.Ln, bias=EPS, scale=1.0 / 128.0),
                    reads=[PB[7]], writes=[Bep])
                c.I("act", "activation", dict(out=rsd[:], in_=rsd[:], func=AF.Exp, scale=-0.5), reads=[Bep], writes=[Bep])
                so = nq[0] % 2
                nq[0] += 1
                c.I("dve", "scalar_tensor_tensor", dict(out=ob[so][:], in0=od[:], scalar=sc_[:, 5:6], in1=rsd[:], op0=ALU.mult, op1=ALU.mult),
                    reads=[Bep, Bsc], writes=[Bob[so]])
                c.dma(AT_lat[h * 128:(h + 1) * 128, b * SEQ + q0:b * SEQ + q0 + 512], ob[so][:], reads=[Bob[so]])
            defer.append((n + 1 + DL[0], ep_a))
            defer.append((n + 1 + DL[1], ep_b))
            defer.append((n + 1 + DL[2], ep_c))

        N = len(steps)
        for n in range(N):
            emit_S(n)
            while defer and defer[0][0] <= n:
                defer.pop(0)[1]()
            if n >= 1:
                emit_OL(n - 1)
            idx, _, _, qt, j = steps[n]
            if qt == 0 and j == 0 and idx + 1 < len(bhs):
                load_bh(idx + 1, *bhs[idx + 1])
        emit_OL(N - 1)
        while defer:
            defer.pop(0)[1]()
        c.barrier()
        ar.reset(m0)

    def stage_winattn(l, ctx_out):
        c.stage = "L%d_wattn" % l
        m0 = ar.mark()
        p = "l%d_" % l
        NKCH = (SEQ + NCTX) // 128
        NQB = SEQ // 128
        Bsc = Buf("wsc")
        msk32 = ar.alloc("msk32", [128, 1024], F32)
        mskb = ar.alloc("mskb", [128, 2, 512], BF16)
        c.dma(msk32[:], mask_d, writes=[Bsc])
        c.I("dve", "tensor_copy", dict(out=mskb[:].rearrange("p a n -> p (a n)"), in_=msk32[:]), reads=[Bsc], writes=[Bsc])
        esk = ar.alloc("esk", [128, 16], F32)
        c.I("act", "activation", dict(out=esk[:], in_=cvc(p + "sink", 0, 16), func=AF.Exp), reads=[Bcv], writes=[Bsc])
        zer = ar.alloc("zer", [64, 128], F32)
        c.I("pool", "memset", dict(ap=zer[:], constant=0.0), writes=[Bsc])
        ES = ar.alloc("ES", [64, 4, 4, 128], F32)
        for hk in range(4):
            for g in range(4):
                c.I("dve", "tensor_scalar", dict(out=ES[:, hk, g, :], in0=zer[:], scalar1=esk[0:64, hk * 4 + g:hk * 4 + g + 1], scalar2=None, op0=ALU.add),
                    reads=[Bsc], writes=[Bsc])
        kth = [ar.alloc("kth%d" % i, [128, SEQ + NCTX], BF16) for i in range(2)]
        vh = [ar.alloc("vh%d" % i, [128, NKCH, 128], BF16) for i in range(2)]
        qg = [ar.alloc("qg%d" % i, [128, 4, SEQ + NCTX], BF16) for i in range(2)]
        Bkv = bufs(2, "kvq")
        for i_ in range(2):
            c.I("pool", "memset", dict(ap=kth[i_][64:128, :], constant=0.0), writes=[Bkv[i_]])
            c.I("pool", "memset", dict(ap=qg[i_][64:128, :, :], constant=0.0), writes=[Bkv[i_]])
            c.I("pool", "memset", dict(ap=vh[i_][:, :, 64:128], constant=1.0), writes=[Bkv[i_]])
        E = [ar.alloc("E%d" % i, [128, 512], BF16) for i in range(4)]
        BE = bufs(4, "E")
        lt = [ar.alloc("lt%d" % i, [64, 512], F32) for i in range(2)]
        Blt = bufs(2, "lt")
        ob = [ar.alloc("ob%d" % i, [64, 4, 128], BF16) for i in range(3)]
        Bob = bufs(3, "ob")

        def load_bh(idx, b, hk):
            s_ = idx % 2
            c.dma(kth[s_][0:64, 0:SEQ], KT_lat[hk * 64:(hk + 1) * 64, b * SEQ:(b + 1) * SEQ], writes=[Bkv[s_]])
            c.dma(kth[s_][0:64, SEQ:SEQ + NCTX], KT_ctx[hk * 64:(hk + 1) * 64, b * NCTX:(b + 1) * NCTX], writes=[Bkv[s_]])
            c.dma(vh[s_][:, 0:SEQ // 128, 0:64], V_lat[b * SEQ:(b + 1) * SEQ, hk * 64:(hk + 1) * 64].rearrange("(j p) e -> p j e", p=128), writes=[Bkv[s_]])
            c.dma(vh[s_][:, SEQ // 128:NKCH, 0:64], V_ctx[b * NCTX:(b + 1) * NCTX, hk * 64:(hk + 1) * 64].rearrange("(j p) e -> p j e", p=128), writes=[Bkv[s_]])
            c.dma(qg[s_][0:64, :, 0:SEQ], QT_lat[hk * 256:(hk + 1) * 256, b * SEQ:(b + 1) * SEQ].rearrange("(g d) n -> d g n", d=64), writes=[Bkv[s_]])
            if ctx_out:
                c.dma(qg[s_][0:64, :, SEQ:SEQ + NCTX], QT_ctx[hk * 256:(hk + 1) * 256, b * NCTX:(b + 1) * NCTX].rearrange("(g d) n -> d g n", d=64), writes=[Bkv[s_]])

        bhs = [(b, hk) for b in range(NB) for hk in range(4)]
        load_bh(0, *bhs[0])
        steps = []
        nblk = 0
        for idx, (b, hk) in enumerate(bhs):
            blocks = [("lat", qb) for qb in range(NQB)]
            if ctx_out:
                blocks += [("ctx", qb) for qb in range(NCTX // 128)]
            for (bk, qb) in blocks:
                if bk == "lat":
                    qc0 = qb * 128
                    kcs = []
                    if qb > 0:
                        kcs.append((qb - 1, 0))
                    kcs.append((qb, None))
                    if qb < NQB - 1:
                        kcs.append((qb + 1, 1))
                    kcs += [(NQB, None), (NQB + 1, None)]
                else:
                    qc0 = SEQ + qb * 128
                    kcs = [(NQB, None), (NQB + 1, None)]
                for ki, (kc, mk) in enumerate(kcs):
                    steps.append(dict(idx=idx, b=b, hk=hk, bk=bk, qb=qb, qc0=qc0, kc=kc, mk=mk, first=(ki == 0),
                                      last=(ki == len(kcs) - 1), ol=nblk % 2, nblk=nblk, newidx=(ki == 0 and (bk, qb) == blocks[0])))
                nblk += 1

        def emit_S(n):
            st = steps[n]
            s_ = st["idx"] % 2
            sb_ = n % 4
            es = n % 4
            kc, qc0 = st["kc"], st["qc0"]
            c.I("pe", "matmul", dict(out=psum[:, sb_, :], lhsT=kth[s_][:, kc * 128:(kc + 1) * 128], rhs=qg[s_][:, :, qc0:qc0 + 128],
                                     start=True, stop=True), reads=[Bkv[s_]], writes=[PB[sb_]], inc=True)
            c.I("act", "activation", dict(out=E[es][:], in_=psum[:, sb_, :], func=AF.Exp, scale=0.125), reads=[PB[sb_]], writes=[BE[es]])
            if st["mk"] is not None:
                c.I("pool", "tensor_tensor", dict(out=E[es][:], in0=E[es][:], in1=mskb[:, st["mk"], :], op=ALU.mult), reads=[BE[es], Bsc], writes=[BE[es]])

        def emit_OL(n):
            st = steps[n]
            s_ = st["idx"] % 2
            es = n % 4
            ol = st["ol"]
            bo_ = 4 + ol
            kc, hk, b, qb = st["kc"], st["hk"], st["b"], st["qb"]
            c.I("pe", "matmul", dict(out=psum[:, bo_, :], lhsT=vh[s_][:, kc, :], rhs=E[es][:], start=st["first"], stop=st["last"]),
                reads=[Bkv[s_], BE[es]], writes=[PB[bo_]], inc=st["last"])
            if not st["last"]:
                return
            c.I("dve", "tensor_tensor", dict(out=lt[ol][:], in0=psum[64:128, bo_, :], in1=ES[:, hk, :, :].rearrange("p g q -> p (g q)"), op=ALU.add),
                reads=[PB[bo_], Bsc], writes=[Blt[ol]])
            c.I("act", "activation", dict(out=lt[ol][:], in_=lt[ol][:], func=AF.Ln), reads=[Blt[ol]], writes=[Blt[ol]])
            c.I("act", "activation", dict(out=lt[ol][:], in_=lt[ol][:], func=AF.Exp, scale=-1.0), reads=[Blt[ol]], writes=[Blt[ol]])
            so = st["nblk"] % 3
            c.I("dve", "tensor_tensor", dict(out=ob[so][:].rearrange("p g q -> p (g q)"), in0=psum[0:64, bo_, :], in1=lt[ol][:], op=ALU.mult),
                reads=[PB[bo_], Blt[ol]], writes=[Bob[so]])
            if st["bk"] == "lat":
                dst = AT_lat[hk * 256:(hk + 1) * 256, b * SEQ + qb * 128:b * SEQ + (qb + 1) * 128]
            else:
                dst = AT_ctx[hk * 256:(hk + 1) * 256, b * NCTX + qb * 128:b * NCTX + (qb + 1) * 128]
            c.dma(dst.rearrange("(g d) n -> d g n", d=64), ob[so][:], reads=[Bob[so]])

        N = len(steps)
        pend_load = None
        for n in range(N):
            emit_S(n)
            if n >= 2:
                emit_OL(n - 2)
                if pend_load is not None and pend_load[0] <= n - 2:
                    load_bh(pend_load[1], *bhs[pend_load[1]])
                    pend_load = None
            st = steps[n]
            if st["newidx"] and st["idx"] + 1 < len(bhs):
                pend_load = (n - 1, st["idx"] + 1)
                if n == 0:
                    load_bh(1, *bhs[1])
                    pend_load = None
        emit_OL(N - 2)
        emit_OL(N - 1)
        c.barrier()
        ar.reset(m0)

    def stage_outproj(l, wname, src_lat, src_ctx, dst_lat, dst_ctx, with_ctx):
        c.stage = "L%d_oproj" % l
        m0 = ar.mark()
        p = "l%d_" % l
        wo = ar.alloc("wo", [128, KD, D], BF16)
        Bwo = Buf("wo")
        m1 = ar.mark()
        stg = [ar.alloc("stg%d" % i, [128, 2048], F32) for i in range(4)]
        Bstg = bufs(4, "stg")
        load_weight(wo, Bwo, W[p + wname], KD, D, stg, Bstg)
        c.barrier()
        ar.reset(m1)
        at = [ar.alloc("at%d" % i, [128, KD, TQ], BF16) for i in range(2)]
        Bat = bufs(2, "at")
        hh = [ar.alloc("hh%d" % i, [128, KD, TQ], F32) for i in range(2)]
        Bhh = bufs(2, "hh")
        yo = [ar.alloc("yo%d" % i, [128, KD, TQ], F32) for i in range(2)]
        Byo = bufs(2, "yo")
        tiles = []
        for b in range(NB):
            for i in range(SEQ // TQ):
                tiles.append(dict(kind="lat", tok=b, g0=b * SEQ + i * TQ))
        if with_ctx:
            tiles.append(dict(kind="ctx", tok=2, g0=0))
        srcA = {"lat": AT_lat, "ctx": AT_ctx}
        srcH = {"lat": src_lat, "ctx": src_ctx}
        dstH = {"lat": dst_lat, "ctx": dst_ctx}

        def load(i):
            t = tiles[i]
            g0 = t["g0"]
            c.dma(at[i % 2][:], srcA[t["kind"]][:, g0:g0 + TQ].rearrange("(k p) n -> p k n", p=128), writes=[Bat[i % 2]])
            c.dma(hh[i % 2][:], srcH[t["kind"]][:, g0:g0 + TQ].rearrange("(k p) n -> p k n", p=128), writes=[Bhh[i % 2]])
        load(0)
        for i, t in enumerate(tiles):
            if i + 1 < len(tiles):
                load(i + 1)
            for nn in range(KD):
                bo = nn % 8
                mm_group(psum[:, bo, 0:TQ], PB[bo],
                         [(wo[:, m, nn * 128:(nn + 1) * 128], at[i % 2][:, m, :], [Bwo, Bat[i % 2]]) for m in range(KD)])
                c.I("dve", "scalar_tensor_tensor", dict(out=yo[i % 2][:, nn, :], in0=psum[:, bo, 0:TQ], scalar=mod_ap(l, 2, nn, t["tok"]),
                                                        in1=hh[i % 2][:, nn, :], op0=ALU.mult, op1=ALU.add),
                    reads=[PB[bo], Bhh[i % 2], Bmod], writes=[Byo[i % 2]])
            g0 = t["g0"]
            c.dma(dstH[t["kind"]][:, g0:g0 + TQ].rearrange("(k p) n -> p k n", p=128), yo[i % 2][:], reads=[Byo[i % 2]])
        c.barrier()
        ar.reset(m0)

    stage_ada()
    cur_lat, cur_ctx = xT, ctxT
    for li, l in enumerate(layers if stop != "ada" else ()):
        kind = _layer_kind(l)
        ctx_after = any(j % 3 != 0 for j in range(l + 1, 4))
        lastl = (li == len(layers) - 1)
        mid_lat, mid_ctx = hbuf[0], hcbuf[0]
        out_lat = yT if lastl else hbuf[1]
        out_ctx = (ycT if debug else hcbuf[1]) if lastl else hcbuf[1]
        if kind == 0:
            stage_sconv(l, cur_lat, cur_ctx, (out_lat if stop == "mix" else mid_lat), (out_ctx if stop == "mix" else mid_ctx), ctx_after)
        elif kind == 1:
            stage_qkv(l, 1, cur_lat, cur_ctx, ctx_after)
            if stop != "qkv":
                stage_winattn(l, ctx_after)
                stage_outproj(l, "wa_out", cur_lat, cur_ctx, (out_lat if stop == "mix" else mid_lat), (out_ctx if stop == "mix" else mid_ctx), ctx_after)
        else:
            stage_qkv(l, 2, cur_lat, cur_ctx, ctx_after)
            if stop != "qkv":
                stage_diffattn(l)
                stage_outproj(l, "da_out", cur_lat, cur_ctx, (out_lat if stop == "mix" else mid_lat), (out_ctx if stop == "mix" else mid_ctx), ctx_after)
        if stop in ("mix", "qkv"):
            break
        stage_ffn(l, mid_lat, mid_ctx, out_lat, out_ctx, ctx_after)
        cur_lat, cur_ctx = out_lat, out_ctx
    c.emit()
    nc._inst_stage = c.inst_stage
    return nc


def _prep_inputs(inp, core):
    b0 = core * NB
    m = {}
    x = inp["x"][b0:b0 + NB]
    m["xT"] = np.ascontiguousarray(x.reshape(NB * SEQ, D).T)
    cx = inp["ctx"][b0:b0 + NB]
    m["ctxT"] = np.ascontiguousarray(cx.reshape(NB * NCTX, D).T)
    c3 = np.zeros((4, D), np.float32)
    c3[0:NB] = inp["c"][b0:b0 + NB]
    c3[2] = inp["c_ctx"]
    m["c3T"] = np.ascontiguousarray(c3.T)
    return m


def _run(inp, layers=(0, 1, 2, 3), debug=False, cores=N_CORES, stop=None, trace=False):
    nc = build_program(layers=layers, debug=debug, stop=stop)
    cvec = _pack_cvec(inp)
    cosT, sinT, rperm = _rope_tables()
    shared = {"cvec": cvec, "cosT": cosT, "sinT": sinT, "rperm": rperm, "masks": _band_masks()}
    for l in range(4):
        p = "l%d_" % l
        names = ["ada_w", "ffn_up", "ffn_down"]
        k = _layer_kind(l)
        names += {0: ["sc_in", "sc_out"], 1: ["wa_qkv", "wa_out"], 2: ["da_qkv", "da_out"]}[k]
        for nme in names:
            shared[p + nme] = np.ascontiguousarray(inp[p + nme], dtype=np.float32)
    in_maps = []
    for core in range(cores):
        m = dict(shared)
        m.update(_prep_inputs(inp, core))
        in_maps.append(m)
    res = run_bass_kernel_spmd(nc, in_maps, core_ids=list(range(cores)), trace=trace)
    res._inst_stage = getattr(nc, "_inst_stage", {})
    return res


_INPUT_NAMES = (
    "x", "c", "ctx", "c_ctx",
    "l0_ada_w", "l0_ada_b", "l0_norm1", "l0_norm2", "l0_sc_in", "l0_sc_conv", "l0_sc_out",
    "l0_ffn_up", "l0_ffn_conv_w", "l0_ffn_conv_b", "l0_ffn_down",
    "l1_ada_w", "l1_ada_b", "l1_norm1", "l1_norm2", "l1_wa_qkv", "l1_wa_qnorm", "l1_wa_knorm", "l1_wa_sink", "l1_wa_out",
    "l1_ffn_up", "l1_ffn_conv_w", "l1_ffn_conv_b", "l1_ffn_down",
    "l2_ada_w", "l2_ada_b", "l2_norm1", "l2_norm2", "l2_da_qkv", "l2_da_qnorm", "l2_da_knorm",
    "l2_da_lq1", "l2_da_lk1", "l2_da_lq2", "l2_da_lk2", "l2_da_subln", "l2_da_out",
    "l2_ffn_up", "l2_ffn_conv_w", "l2_ffn_conv_b", "l2_ffn_down",
    "l3_ada_w", "l3_ada_b", "l3_norm1", "l3_norm2", "l3_sc_in", "l3_sc_conv", "l3_sc_out",
    "l3_ffn_up", "l3_ffn_conv_w", "l3_ffn_conv_b", "l3_ffn_down",
)


def kernel(**inputs):
    inp = {k: np.asarray(inputs[k]) for k in _INPUT_NAMES}
    res = _run(inp)
    out = np.empty((N_CORES * NB, SEQ, D), np.float32)
    for core in range(N_CORES):
        yT = res.results[core]["yT"]
        out[core * NB:(core + 1) * NB] = yT.T.reshape(NB, SEQ, D)
    return out
```
